# Optimizing a Trainium2 kernel written in Bass

```python
import math
import jax, jax.numpy as jnp
from jax import lax
import numpy as np

D_MODEL = 2048
BATCH = 4
SEQ = 2048
DEPTH = 2
DEC_BATCH = 128
DEC_SEQ = 4
PAST_LEN = 16384
PAGE_SIZE = 128

MIX_WIDTH = 2 * D_MODEL
M_WIDTH = MIX_WIDTH // 4
M_HEADS = 4
M_DH = M_WIDTH // M_HEADS
M_CHUNK = 64
S_WIDTH = MIX_WIDTH // 2
S_DH = 64
S_HEADS = S_WIDTH // S_DH
S_GROUPS = 4
S_HPG = S_HEADS // S_GROUPS
S_STATE = 128
S_CHUNK = 128
S_CONV_DIM = S_WIDTH + 2 * S_GROUPS * S_STATE
C_WIDTH = MIX_WIDTH // 4
C_GROUPS = 4
C_DG = C_WIDTH // C_GROUPS
C_CHUNK = 128
CONV_K = 4
EPS = 1e-6
PROJ_SIZES = (M_WIDTH, M_WIDTH, S_WIDTH, S_CONV_DIM, S_HEADS, C_WIDTH, C_WIDTH, C_WIDTH)
PROJ_WIDTH = sum(PROJ_SIZES)

kernel_name = "hymba_mlstm_ssd_chunkmlp_step"


def rmsnorm(x, g):
    xf = x.astype(jnp.float32)
    xf = xf * lax.rsqrt(jnp.mean(xf * xf, axis=-1, keepdims=True) + EPS)
    return xf.astype(x.dtype) * g


def group_rmsnorm(x, g, groups):
    shp = x.shape
    xg = x.reshape(shp[:-1] + (groups, shp[-1] // groups))
    return rmsnorm(xg, g.reshape(groups, -1)).reshape(shp)


def causal_dwconv(x, buf, w, b):
    xf = jnp.concatenate([buf.astype(x.dtype), x], axis=1)
    y = lax.conv_general_dilated(xf, w[:, None, :].astype(x.dtype), window_strides=(1,), padding='VALID',
                                 dimension_numbers=('NWC', 'WIO', 'NWC'), feature_group_count=x.shape[-1])
    return jax.nn.silu(y + b), xf[:, -(CONV_K - 1):]


def mlstm_chunked(q, k, v, i_pre, logf, C0, n0, m0):
    Bsz, T, H, D = q.shape
    L = math.gcd(T, M_CHUNK)
    NC = T // L
    k = k * (D ** -0.5)
    def to_chunks(a):
        a = a.reshape((Bsz, NC, L) + a.shape[2:])
        return jnp.moveaxis(jnp.moveaxis(a, 1, 0), 2, 3)
    qc, kc, vc, ic, fc = to_chunks(q), to_chunks(k), to_chunks(v), to_chunks(i_pre), to_chunks(logf)
    causal = jnp.tril(jnp.ones((L, L), bool))

    def step(carry, inp):
        C, n, m = carry
        qk_, kk, vk, ik, fk = inp
        b = jnp.cumsum(fk, axis=-1)
        a = b + m[..., None]
        dmat = jnp.where(causal, b[..., :, None] - b[..., None, :] + ik[..., None, :], -jnp.inf)
        m_new = jnp.maximum(a, jnp.max(dmat, axis=-1))
        w_inter = jnp.exp(a - m_new)
        w_intra = jnp.exp(dmat - m_new[..., None])
        s = jnp.einsum('bhtd,bhsd->bhts', qk_, kk) * w_intra
        num = jnp.einsum('bhts,bhsd->bhtd', s, vk) + w_inter[..., None] * jnp.einsum('bhvk,bhtk->bhtv', C, qk_)
        den = jnp.sum(s, axis=-1) + w_inter * jnp.einsum('bhk,bhtk->bht', n, qk_)
        h = num / jnp.maximum(jnp.abs(den), jnp.exp(-m_new))[..., None]
        wl_inter = w_inter[..., -1]
        wl = w_intra[..., -1, :]
        C_new = wl_inter[..., None, None] * C + jnp.einsum('bhs,bhsv,bhsk->bhvk', wl, vk, kk)
        n_new = wl_inter[..., None] * n + jnp.einsum('bhs,bhsk->bhk', wl, kk)
        return (C_new, n_new, m_new[..., -1]), h

    (C1, n1, m1), hs = lax.scan(step, (C0, n0, m0), (qc, kc, vc, ic, fc))
    h = jnp.moveaxis(jnp.moveaxis(hs, 0, 1), 3, 2).reshape(Bsz, T, H, D)
    return h, C1, n1, m1


def ssd_chunked(x, dt, A, Bm, Cm, h0):
    Bsz, T = x.shape[:2]
    L = math.gcd(T, S_CHUNK)
    NC = T // L
    def to_chunks(a):
        return jnp.moveaxis(a.reshape((Bsz, NC, L) + a.shape[2:]), 1, 0)
    xc = to_chunks(x.reshape(Bsz, T, S_GROUPS, S_HPG, S_DH))
    dtc = to_chunks(dt.reshape(Bsz, T, S_GROUPS, S_HPG))
    bc, cc = to_chunks(Bm), to_chunks(Cm)
    Ag = A.reshape(S_GROUPS, S_HPG)
    causal = jnp.tril(jnp.ones((L, L), bool))[None, :, :, None, None]

    def step(h, inp):
        xk, dtk, bk, ck = inp
        cs = jnp.cumsum(dtk * Ag, axis=1)
        decay = jnp.exp(jnp.where(causal, cs[:, :, None] - cs[:, None, :], -jnp.inf))
        cb = jnp.einsum('btgn,bsgn->btsg', ck, bk)
        mix = cb[..., None] * decay * dtk[:, None]
        y = jnp.einsum('btsgh,bsghp->btghp', mix, xk)
        y = y + jnp.exp(cs)[..., None] * jnp.einsum('btgn,bghpn->btghp', ck, h)
        w_end = jnp.exp(cs[:, -1:] - cs) * dtk
        h = jnp.exp(cs[:, -1])[..., None, None] * h + jnp.einsum('bsgh,bsgn,bsghp->bghpn', w_end, bk, xk)
        return h, y

    hT, ys = lax.scan(step, h0.reshape(Bsz, S_GROUPS, S_HPG, S_DH, S_STATE), (xc, dtc, bc, cc))
    y = jnp.moveaxis(ys, 0, 1).reshape(Bsz, T, S_HEADS, S_DH)
    return y, hT.reshape(Bsz, S_HEADS, S_DH, S_STATE)


def chunk_mlp(u, v, w_s, b_s):
    Bsz, T = u.shape[:2]
    blk = T if T < C_CHUNK else C_CHUNK
    NC = -(-T // blk)
    pad = NC * blk - T
    vp = jnp.pad(v, ((0, 0), (0, pad), (0, 0))).reshape(Bsz, NC, blk, C_GROUPS, C_DG)
    w = w_s[:, :blk, :blk] * jnp.tril(jnp.ones((blk, blk), w_s.dtype))
    mixed = jnp.einsum('gts,bcsgd->bctgd', w, vp) + b_s[:, :blk].T[:, :, None]
    mixed = mixed.reshape(Bsz, NC * blk, C_WIDTH)[:, :T]
    return u * mixed


def mixer_layer(x, mconv_buf, C0, n0, m0, sconv_buf, h0,
                norm_g, w_in, m_conv_w, m_conv_b, m_w_qk, m_w_vo, m_b_o, m_w_gate, m_b_gate, m_norm_g,
                s_conv_w, s_conv_b, s_dt_bias, s_A_log, s_D, s_norm_g,
                c_v_norm_g, c_w_s, c_b_s, w_out):
    f32 = jnp.float32
    Bsz, T, _ = x.shape
    h = rmsnorm(x, norm_g)
    proj = jnp.einsum('btd,de->bte', h, w_in)
    splits = np.cumsum(PROJ_SIZES)[:-1].tolist()
    xm, zm, zs, xbc, dt_raw, u, v, zc = jnp.split(proj, splits, axis=-1)

    xmc, mconv_new = causal_dwconv(xm, mconv_buf, m_conv_w, m_conv_b)
    q, k = jnp.split(jnp.einsum('bthd,hde->bthe', xmc.reshape(Bsz, T, M_HEADS, M_DH), m_w_qk), 2, axis=-1)
    vm, o_pre = jnp.split(jnp.einsum('bthd,hde->bthe', xm.reshape(Bsz, T, M_HEADS, M_DH), m_w_vo), 2, axis=-1)
    gates = jnp.einsum('bte,eg->btg', jnp.concatenate([q, k, vm], axis=-1).reshape(Bsz, T, 3 * M_WIDTH), m_w_gate) + m_b_gate
    i_pre, f_pre = jnp.split(gates.astype(f32), 2, axis=-1)
    hm, C1, n1, m1 = mlstm_chunked(q.astype(f32), k.astype(f32), vm.astype(f32), i_pre, jax.nn.log_sigmoid(f_pre),
                                   C0.astype(f32), n0.astype(f32), m0.astype(f32))
    hm = jax.nn.sigmoid(o_pre.astype(f32) + m_b_o.reshape(M_HEADS, M_DH)) * hm
    hm = rmsnorm(hm, m_norm_g.reshape(M_HEADS, M_DH)).reshape(Bsz, T, M_WIDTH).astype(x.dtype)
    out_m = hm * jax.nn.silu(zm)

    xbc_c, sconv_new = causal_dwconv(xbc, sconv_buf, s_conv_w, s_conv_b)
    xs, Bm, Cm = jnp.split(xbc_c, [S_WIDTH, S_WIDTH + S_GROUPS * S_STATE], axis=-1)
    dt = jax.nn.softplus(dt_raw.astype(f32) + s_dt_bias)
    A = -jnp.exp(s_A_log.astype(f32))
    xs_h = xs.reshape(Bsz, T, S_HEADS, S_DH).astype(f32)
    ys, h1 = ssd_chunked(xs_h, dt, A, Bm.reshape(Bsz, T, S_GROUPS, S_STATE).astype(f32),
                         Cm.reshape(Bsz, T, S_GROUPS, S_STATE).astype(f32), h0.astype(f32))
    ys = (ys + s_D[:, None] * xs_h).reshape(Bsz, T, S_WIDTH) * jax.nn.silu(zs.astype(f32))
    out_s = group_rmsnorm(ys, s_norm_g, S_GROUPS).astype(x.dtype)

    vn = group_rmsnorm(v, c_v_norm_g, C_GROUPS)
    out_c = chunk_mlp(u, vn, c_w_s, c_b_s) * jax.nn.silu(zc)

    mix = jnp.concatenate([out_m, out_s, out_c], axis=-1)
    y = x + jnp.einsum('bte,ed->btd', mix, w_out)
    return y, (C1.astype(C0.dtype), n1.astype(n0.dtype), m1.astype(m0.dtype), mconv_new,
               h1.astype(h0.dtype), sconv_new, vn)


def setup_inputs(seed: int = 0) -> dict:
    key = jax.random.key(seed)
    ks = iter(jax.random.split(key, 40))
    f32 = jnp.float32

    def nrm(shape, scale):
        return jax.random.normal(next(ks), shape, f32) * scale

    x_prompt = nrm((BATCH, SEQ, D_MODEL), 1.0)
    x_sample = nrm((DEC_BATCH, DEC_SEQ, D_MODEL), 1.0)
    state_mlstm_C = nrm((DEPTH, DEC_BATCH, M_HEADS, M_DH, M_DH), 0.05)
    state_mlstm_n = nrm((DEPTH, DEC_BATCH, M_HEADS, M_DH), 0.05)
    state_mlstm_m = jax.random.uniform(next(ks), (DEPTH, DEC_BATCH, M_HEADS), f32, -1.0, 2.0)
    state_mlstm_conv = nrm((DEPTH, DEC_BATCH, CONV_K - 1, M_WIDTH), 1.0)
    state_ssm = nrm((DEPTH, DEC_BATCH, S_HEADS, S_DH, S_STATE), 0.1)
    state_ssm_conv = nrm((DEPTH, DEC_BATCH, CONV_K - 1, S_CONV_DIM), 1.0)

    norm_g = 1.0 + nrm((DEPTH, D_MODEL), 0.02)
    w_in = nrm((DEPTH, D_MODEL, PROJ_WIDTH), D_MODEL ** -0.5)
    m_conv_w = nrm((DEPTH, CONV_K, M_WIDTH), CONV_K ** -0.5)
    m_conv_b = nrm((DEPTH, M_WIDTH), 0.02)
    m_w_qk = nrm((DEPTH, M_HEADS, M_DH, 2 * M_DH), M_DH ** -0.5)
    m_w_vo = nrm((DEPTH, M_HEADS, M_DH, 2 * M_DH), M_DH ** -0.5)
    m_b_o = nrm((DEPTH, M_WIDTH), 0.02)
    m_w_gate = nrm((DEPTH, 3 * M_WIDTH, 2 * M_HEADS), (3 * M_WIDTH) ** -0.5)
    f_bias = jnp.broadcast_to(jnp.linspace(3.0, 6.0, M_HEADS, dtype=f32), (DEPTH, M_HEADS))
    m_b_gate = jnp.concatenate([nrm((DEPTH, M_HEADS), 0.1), f_bias + nrm((DEPTH, M_HEADS), 0.1)], axis=-1)
    m_norm_g = 1.0 + nrm((DEPTH, M_WIDTH), 0.02)
    s_conv_w = nrm((DEPTH, CONV_K, S_CONV_DIM), CONV_K ** -0.5)
    s_conv_b = nrm((DEPTH, S_CONV_DIM), 0.02)
    dt0 = jnp.exp(jax.random.uniform(next(ks), (DEPTH, S_HEADS), f32, math.log(1e-3), math.log(1e-1)))
    s_dt_bias = dt0 + jnp.log(-jnp.expm1(-dt0))
    s_A_log = jnp.log(jax.random.uniform(next(ks), (DEPTH, S_HEADS), f32, 1.0, 16.0))
    s_D = 1.0 + nrm((DEPTH, S_HEADS), 0.1)
    s_norm_g = 1.0 + nrm((DEPTH, S_WIDTH), 0.02)
    c_v_norm_g = 1.0 + nrm((DEPTH, C_WIDTH), 0.02)
    c_w_s = nrm((DEPTH, C_GROUPS, C_CHUNK, C_CHUNK), C_CHUNK ** -0.5)
    c_b_s = 1.0 + nrm((DEPTH, C_GROUPS, C_CHUNK), 0.02)
    w_out = nrm((DEPTH, MIX_WIDTH, D_MODEL), (2 * DEPTH * MIX_WIDTH) ** -0.5)
    final_norm_g = 1.0 + nrm((D_MODEL,), 0.02)
    return {"x_prompt": x_prompt, "x_sample": x_sample,
            "state_mlstm_C": state_mlstm_C, "state_mlstm_n": state_mlstm_n, "state_mlstm_m": state_mlstm_m,
            "state_mlstm_conv": state_mlstm_conv, "state_ssm": state_ssm, "state_ssm_conv": state_ssm_conv,
            "norm_g": norm_g, "w_in": w_in, "m_conv_w": m_conv_w, "m_conv_b": m_conv_b,
            "m_w_qk": m_w_qk, "m_w_vo": m_w_vo, "m_b_o": m_b_o, "m_w_gate": m_w_gate, "m_b_gate": m_b_gate,
            "m_norm_g": m_norm_g, "s_conv_w": s_conv_w, "s_conv_b": s_conv_b, "s_dt_bias": s_dt_bias,
            "s_A_log": s_A_log, "s_D": s_D, "s_norm_g": s_norm_g, "c_v_norm_g": c_v_norm_g,
            "c_w_s": c_w_s, "c_b_s": c_b_s, "w_out": w_out, "final_norm_g": final_norm_g}


def reference(x_prompt, x_sample, state_mlstm_C, state_mlstm_n, state_mlstm_m, state_mlstm_conv,
              state_ssm, state_ssm_conv, norm_g, w_in, m_conv_w, m_conv_b, m_w_qk, m_w_vo, m_b_o,
              m_w_gate, m_b_gate, m_norm_g, s_conv_w, s_conv_b, s_dt_bias, s_A_log, s_D, s_norm_g,
              c_v_norm_g, c_w_s, c_b_s, w_out, final_norm_g):
    dtype = x_prompt.dtype
    bp = x_prompt.shape[0]
    yp, ysm = x_prompt, x_sample
    outs_p, outs_s = [], []
    for l in range(DEPTH):
        weights = (norm_g[l], w_in[l], m_conv_w[l], m_conv_b[l], m_w_qk[l], m_w_vo[l], m_b_o[l],
                   m_w_gate[l], m_b_gate[l], m_norm_g[l], s_conv_w[l], s_conv_b[l], s_dt_bias[l],
                   s_A_log[l], s_D[l], s_norm_g[l], c_v_norm_g[l], c_w_s[l], c_b_s[l], w_out[l])
        yp, sp = mixer_layer(yp,
                             jnp.zeros((bp, CONV_K - 1, M_WIDTH), dtype),
                             jnp.zeros((bp, M_HEADS, M_DH, M_DH), dtype),
                             jnp.zeros((bp, M_HEADS, M_DH), dtype),
                             jnp.zeros((bp, M_HEADS), dtype),
                             jnp.zeros((bp, CONV_K - 1, S_CONV_DIM), dtype),
                             jnp.zeros((bp, S_HEADS, S_DH, S_STATE), dtype),
                             *weights)
        ysm, ss = mixer_layer(ysm, state_mlstm_conv[l], state_mlstm_C[l], state_mlstm_n[l], state_mlstm_m[l],
                              state_ssm_conv[l], state_ssm[l], *weights)
        outs_p.append(sp)
        outs_s.append(ss)
    p_C, p_n, p_m, p_mconv, p_ssm, p_sconv = [jnp.stack([s[i] for s in outs_p]) for i in range(6)]
    s_C, s_n, s_m, s_mconv, s_ssm, s_sconv, s_cv = [jnp.stack([s[i] for s in outs_s]) for i in range(7)]
    y_prompt = rmsnorm(yp, final_norm_g)
    y_sample = rmsnorm(ysm, final_norm_g)
    return (y_prompt, y_sample, p_C, p_n, p_m, p_mconv, p_ssm, p_sconv,
            s_C, s_n, s_m, s_mconv, s_ssm, s_sconv, s_cv)
```

```python
import contextlib
import numpy as np
import concourse.bass as bass
import concourse.mybir as mybir
from concourse.bass_utils import run_bass_kernel_spmd

F32 = mybir.dt.float32
BF16 = mybir.dt.bfloat16
ALU = mybir.AluOpType
AF = mybir.ActivationFunctionType
AX = mybir.AxisListType

D = 2048
NDT = 16
EPS = 1e-6
NSEQ = 16
TS = 4
SEGS = (("xm", 0, 1024), ("zm", 1024, 1024), ("zs", 2048, 2048), ("xbc", 4096, 3072),
        ("dt", 7168, 32), ("u", 7200, 1024), ("v", 8224, 1024), ("zc", 9248, 1024))
PROJ_W = 10272
GW = 256
WSPLIT = 4


class Buf:
    __slots__ = ("name", "last_write", "readers", "dma_sem", "dma_cnt", "excl")

    def __init__(self, name, excl=False):
        self.name = name
        self.excl = excl
        self.last_write = None
        self.readers = []
        self.dma_sem = None
        self.dma_cnt = 0


class TT:
    __slots__ = ("t", "b")

    def __init__(self, t, b):
        self.t = t
        self.b = b


class Ctx:
    ENG = ("pe", "dve", "act", "pool", "sp")

    def __init__(self, nc, stack):
        self.nc = nc
        self.stack = stack
        self.q = {e: [] for e in self.ENG}
        self.sem = {e: stack.enter_context(nc.semaphore("s_" + e)) for e in self.ENG}
        self.cnt = {e: 0 for e in self.ENG}
        self.seen = {e: {} for e in self.ENG}
        self.dma_bufs = []
        self.ninst = 0
        self.rec = None

    def _collect(self, e, reads, writes):
        need = {}

        def add(tok):
            if tok is None:
                return
            s, v = tok
            k = id(s)
            if need.get(k, (None, 0))[1] < v:
                need[k] = (s, v)
        own_sem = self.sem[e]
        for b in reads:
            add(b.last_write)
            if b.excl:
                for t in b.readers:
                    if t[0] is not own_sem:
                        add(t)
        for b in writes:
            add(b.last_write)
            for t in b.readers:
                add(t)
        waits = []
        seen = self.seen[e]
        own = id(self.sem[e])
        for k, (s, v) in need.items():
            if seen.get(k, 0) >= v:
                continue
            if e == "pe" and k == own:
                continue
            seen[k] = v
            waits.append((s, v))
        return waits

    def _commit(self, tok, reads, writes):
        for b in writes:
            b.last_write = tok
            b.readers = []
        for b in reads:
            b.readers.append(tok)
            if len(b.readers) > 48:
                best = {}
                for s, v in b.readers:
                    if best.get(id(s), (None, 0))[1] < v:
                        best[id(s)] = (s, v)
                b.readers = list(best.values())

    def op(self, e, fn, reads=(), writes=(), inc=True):
        if self.rec is not None:
            self.rec.append(("op", e, fn, reads, writes, inc))
            return
        waits = self._collect(e, reads, writes)
        sem = self.sem[e]
        if inc:
            self.cnt[e] += 1
            tok = (sem, self.cnt[e])
        else:
            tok = (sem, self.cnt[e] + 1)
        self._commit(tok, reads, writes)
        self.ninst += 1 + len(waits)

        def run(eng, waits=waits, fn=fn, sem=sem, inc=inc):
            for s, v in waits:
                eng.wait_ge(s, v)
            if inc:
                fn(eng).then_inc(sem, 1)
            else:
                fn(eng)
        self.q[e].append(run)

    def dma(self, e, out, in_, sb, reads=(), writes=(), **kw):
        if self.rec is not None:
            self.rec.append(("dma", e, (out, in_, sb, kw), reads, writes, None))
            return
        waits = self._collect(e, reads, writes)
        if sb.dma_sem is None:
            sb.dma_sem = self.stack.enter_context(self.nc.semaphore("d_" + sb.name))
            self.dma_bufs.append(sb)
        sb.dma_cnt += 16
        sem = sb.dma_sem
        tok = (sem, sb.dma_cnt)
        self._commit(tok, reads, writes)
        self.ninst += 1 + len(waits)

        def run(eng, waits=waits, sem=sem, out=out, in_=in_, kw=kw):
            for s, v in waits:
                eng.wait_ge(s, v)
            eng.dma_start(out=out, in_=in_, **kw).then_inc(sem, 16)
        self.q[e].append(run)

    def barrier(self, final=False):
        toks = [(b.dma_sem, b.dma_cnt) for b in self.dma_bufs]
        toks += [(self.sem[e], self.cnt[e]) for e in self.ENG if self.cnt[e] > 0]
        for e in (("sp",) if final else self.ENG):
            waits = []
            for s, v in toks:
                if self.seen[e].get(id(s), 0) >= v:
                    continue
                if s is self.sem[e]:
                    continue
                self.seen[e][id(s)] = v
                waits.append((s, v))

            def run(eng, waits=waits):
                for s, v in waits:
                    eng.wait_ge(s, v)
            self.q[e].append(run)

    def emit(self):
        with self.nc.Block() as block:
            @block.tensor
            def _(eng):
                for f in self.q["pe"]:
                    f(eng)

            @block.vector
            def _(eng):
                for f in self.q["dve"]:
                    f(eng)

            @block.scalar
            def _(eng):
                for f in self.q["act"]:
                    f(eng)

            @block.gpsimd
            def _(eng):
                for f in self.q["pool"]:
                    f(eng)

            @block.sync
            def _(eng):
                for f in self.q["sp"]:
                    f(eng)


class _Stop(Exception):
    pass


def build(NBLK, do_sample=True, stage=None):
    nc = bass.Bass("TRN2", target_bir_lowering=False)
    TP = NBLK * 128

    def din(name, shape):
        return nc.dram_tensor(name, list(shape), F32, kind="ExternalInput").ap()

    def dout(name, shape):
        return nc.dram_tensor(name, list(shape), F32, kind="ExternalOutput").ap()

    xT = din("xT", [D, TP])
    xsT = din("xsT", [D, 64])
    sC = din("sC", [2, NSEQ, 4, 256, 256])
    sn = din("sn", [2, NSEQ * 4, 256])
    smc = din("smc", [2, NSEQ * 3, 1024])
    sssm = din("sssm", [2, NSEQ, 32, 64, 128])
    ssc = din("ssc", [2, NSEQ * 3, 3072])
    gcol_h = din("gcol_h", [128, 3, NDT])
    cwm_h = din("cwm_h", [128, 2, 8, 5])
    cws_h = din("cws_h", [128, 2, 24, 5])
    mng_h = din("mng_h", [128, 2, 8])
    sng_h = din("sng_h", [128, 2, 16])
    mbo_h = din("mbo_h", [128, 2, 8])
    smT = din("smT", [2, 4, NSEQ])
    w_in = din("w_in", [2, D, PROJ_W])
    m_w_qk = din("m_w_qk", [2, 4, 256, 512])
    m_w_vo = din("m_w_vo", [2, 4, 256, 512])
    m_w_gate = din("m_w_gate", [2, 3072, 8])
    m_b_gate = din("m_b_gate", [2, 8])
    s_dt_bias = din("s_dt_bias", [2, 32])
    s_A_log = din("s_A_log", [2, 32])
    s_D = din("s_D", [2, 32])
    c_v_norm_g = din("c_v_norm_g", [2, 1024])
    c_w_sT = din("c_w_sT", [2, 4, 128, 128])
    c_b_s = din("c_b_s", [2, 4, 128])
    w_out = din("w_out", [2, 4096, D])
    cst = din("cst", [128, 13, 128])

    yT = dout("yT", [D, TP])
    ysT = dout("ysT", [D, 64])
    pC = dout("pC", [2, 4, 256, 256])
    pn = dout("pn", [2, 4, 256])
    pm = dout("pm", [2, 4])
    pmc = dout("pmc", [2, 3, 1024])
    pssm = dout("pssm", [2, 32, 64, 128])
    psc = dout("psc", [2, 3, 3072])
    oC = dout("oC", [2, NSEQ, 4, 256, 256])
    on = dout("on", [2, NSEQ, 4, 256])
    om = dout("om", [2, NSEQ, 4])
    omc = dout("omc", [2, NSEQ, 3, 1024])
    ossm = dout("ossm", [2, NSEQ, 32, 64, 128])
    osc = dout("osc", [2, NSEQ, 3, 3072])
    ocv = dout("ocv", [2, NSEQ, 4, 1024])

    NGI = sum((n_ + GW - 1) // GW for (_, _, n_) in SEGS)
    wscr_in = nc.dram_tensor("wscr_in", [2, NGI, 128, 4096], BF16, kind="Internal").ap()
    wscr_out = nc.dram_tensor("wscr_out", [2, NDT, 128, 4096], BF16, kind="Internal").ap()
    phase = {"p": "sample" if do_sample else "none"}

    with contextlib.ExitStack() as st:
        c = Ctx(nc, st)

        def sb(name, shape, dt=F32):
            return TT(st.enter_context(nc.sbuf_tensor(name, list(shape), dt)), Buf(name))

        big = []
        small = []
        for i in range(4):
            t = st.enter_context(nc.psum_tensor("pb%d" % i, [128, 512], F32))
            big.append(TT(t, Buf("pb%d" % i, excl=True)))
        for i in range(4):
            t = st.enter_context(nc.psum_tensor("ph%d" % i, [128, 512], F32))
            small.append(TT(t, Buf("ph%d" % i, excl=True)))
        rot = {"b": 0, "s": 0, "w": 0, "e": 0}

        cpool = {"p": None}
        allbanks = big + small
        CH_POOLS = {"mlstm": {"big": allbanks[0:2], "small": allbanks[2:3]},
                    "ssd": {"big": allbanks[3:5], "small": allbanks[5:6]},
                    "cmlp": {"big": allbanks[6:7], "small": allbanks[7:8]},
                    "inproj": {"big": allbanks[3:4], "small": allbanks[3:8]}}

        def psb():
            if cpool["p"] is not None:
                pl = cpool["p"]
                pl["bi"] = (pl.get("bi", 0) + 1) % len(pl["big"])
                return pl["big"][pl["bi"]]
            rot["b"] = (rot["b"] + 1) % 4
            return big[rot["b"]]

        def pss():
            if cpool["p"] is not None:
                pl = cpool["p"]
                pl["si"] = (pl.get("si", 0) + 1) % len(pl["small"])
                return pl["small"][pl["si"]]
            rot["s"] = (rot["s"] + 1) % 4
            return small[rot["s"]]

        def mm(p, pap, lhsT, rhs, start, stop, R):
            c.op("pe", lambda e: e.matmul(pap, lhsT, rhs, start=start, stop=stop), reads=R, writes=[p.b])

        def act(out, in_, func, R, W, **kw):
            c.op("act", lambda e: e.activation(out, in_, func, **kw), reads=R, writes=W)

        def tt(out, a, b_, op, R, W, eng="dve"):
            c.op(eng, lambda e: e.tensor_tensor(out, a, b_, op), reads=R, writes=W)

        def ts(out, a, s1, s2, op0, op1, R, W, eng="dve"):
            if op1 is None:
                c.op(eng, lambda e: e.tensor_scalar(out, a, s1, None, op0), reads=R, writes=W)
            else:
                c.op(eng, lambda e: e.tensor_scalar(out, a, s1, s2, op0, op1), reads=R, writes=W)

        def stt(out, a, s, b_, op0, op1, R, W):
            c.op("dve", lambda e: e.scalar_tensor_tensor(out, a, s, b_, op0, op1), reads=R, writes=W)

        def cp(out, in_, R, W):
            rot["e"] ^= 1
            if rot["e"]:
                c.op("act", lambda e: e.activation(out, in_, AF.Copy), reads=R, writes=W)
            else:
                c.op("dve", lambda e: e.tensor_copy(out, in_), reads=R, writes=W)

        def dma(e, out, in_, sbuf, R=(), W=()):
            c.dma(e, out, in_, sbuf.b, reads=R, writes=W)

        CS = sb("CS", [128, 13, 128])
        dma("sp", CS.t[:], cst, CS, W=[CS.b])
        I_f = CS.t[:, 0, :]
        U_f = CS.t[:, 1, :]
        MPOS = CS.t[:, 2, :]
        MNEG = CS.t[:, 3, :]
        ONES = CS.t[:, 4, :]
        SELL = {128: CS.t[:, 5, :], 64: CS.t[:, 6, :], 4: CS.t[:, 7, :]}
        SEL4 = [CS.t[:, 9 + h, :] for h in range(4)]
        I_b = sb("I_b", [128, 128], BF16)
        cp(I_b.t[:], I_f, [CS.b], [I_b.b])
        CB = [CS.b, I_b.b]

        gcol = sb("gcol", [128, 3, NDT])
        cwm = sb("cwm", [128, 2, 8, 5])
        cws = sb("cws", [128, 2, 24, 5])
        mng = sb("mng", [128, 2, 8])
        sng = sb("sng", [128, 2, 16])
        mbo = sb("mbo", [128, 2, 8])
        for (tl, src) in ((gcol, gcol_h), (cwm, cwm_h), (cws, cws_h), (mng, mng_h), (sng, sng_h), (mbo, mbo_h)):
            dma("sp", tl.t[:], src, tl, W=[tl.b])
        cvg_bc = sb("cvg_bc", [128, 1024])
        bg_bc = sb("bg_bc", [128, 2, 8])
        dtb_bc = sb("dtb_bc", [128, 2, 32])
        A_bc = sb("A_bc", [128, 2, 32])
        D_bc = sb("D_bc", [128, 2, 32])
        cb_bc = sb("cb_bc", [128, 2, 4, 128])
        for l in range(2):
            dma("sp", bg_bc.t[:, l, :], m_b_gate[l].partition_broadcast(128), bg_bc, W=[bg_bc.b])
            dma("sp", dtb_bc.t[:, l, :], s_dt_bias[l].partition_broadcast(128), dtb_bc, W=[dtb_bc.b])
            dma("sp", A_bc.t[:, l, :], s_A_log[l].partition_broadcast(128), A_bc, W=[A_bc.b])
            dma("sp", D_bc.t[:, l, :], s_D[l].partition_broadcast(128), D_bc, W=[D_bc.b])
            dma("sp", cb_bc.t[:, l, :, :], c_b_s[l].partition_broadcast(128), cb_bc, W=[cb_bc.b])
        act(A_bc.t[:], A_bc.t[:], AF.Exp, [A_bc.b], [A_bc.b])
        ts(A_bc.t[:], A_bc.t[:], -1.0, None, ALU.mult, None, [A_bc.b], [A_bc.b])
        wg = sb("wg", [128, 2, 24, 8], BF16)
        WmT = sb("WmT", [128, 2, 4, 128], BF16)
        wtmp = sb("wtmp", [128, 4, 128])
        for l in range(2):
            dma("pool", wg.t[:, l, :, :], m_w_gate[l].rearrange("(a p) g -> p a g", p=128), wg, W=[wg.b])
            dma("sp", wtmp.t[:], c_w_sT[l].rearrange("g s t -> s g t"), wtmp, W=[wtmp.b])
            for g in range(4):
                tt(WmT.t[:, l, g, :], wtmp.t[:, g, :], U_f, ALU.mult, [wtmp.b, CS.b], [WmT.b])

        xres = sb("xres", [128, NDT, 128])
        hT = sb("hT", [128, NDT, 128], BF16)
        sq = [sb("sq%d" % i, [128, 128]) for i in range(2)]
        rstd_bc = sb("rstd_bc", [128, 128])
        wslot = [sb("wslot%d" % i, [128, 32 * 128], BF16) for i in range(2)]
        wqk = sb("wqk", [128, 8, 512], BF16)
        wvo = sb("wvo", [128, 8, 512], BF16)
        xmin = sb("xmin", [128, 8, 131])
        sbcin = sb("sbcin", [128, 24, 131])
        xmT = sb("xmT", [128, 8, 128], BF16)
        zmS = sb("zmS", [128, 8, 128], BF16)
        xmcT = sb("xmcT", [128, 8, 128], BF16)
        zsS = sb("zsS", [128, 16, 128])
        xsT_ = sb("xsT_", [128, 16, 128])
        BT = sb("BT", [128, 4, 128], BF16)
        CTT = sb("CTT", [128, 4, 128], BF16)
        dtR = sb("dtR", [32, 128])
        uT = sb("uT", [128, 8, 128])
        vT = sb("vT", [128, 8, 128])
        zcS = sb("zcS", [128, 8, 128], BF16)
        mixT = sb("mixT", [128, 32, 128], BF16)
        cacc = sb("cacc", [128, 128])
        mh = [sb("mh%d" % l, [128, 8, 3]) for l in range(2)]
        sh = [sb("sh%d" % l, [128, 24, 3]) for l in range(2)]

        class State:
            pass
        S = []
        for l in range(2):
            s_ = State()
            s_.CT = sb("CT%d" % l, [128, 4, 2, 257])
            s_.HT = sb("HT%d" % l, [128, 32, 64])
            s_.m = sb("m%d" % l, [4, 1])
            S.append(s_)
        CTb = sb("CTb", [128, 4, 2, 257], BF16)
        stgS = TT(S[1].HT.t[:].rearrange("p h d -> p (h d)"), S[1].HT.b)
        stgO = TT(S[1].CT.t[:].rearrange("p a b c -> p (a b c)")[:, 0:2048], S[1].CT.b)
        HTb = sb("HTb", [128, 8, 64], BF16)
        nT_all = sb("nT_all", [128, 2, 64])
        m_all = sb("m_all", [4, NSEQ])
        nrow = sb("nrow", [64, 256])

        stgP = TT(zsS.t[:].rearrange("p a b -> p (a b)"), zsS.b)
        qT = sb("qT", [128, 4, 2, 64], BF16)
        kT = sb("kT", [128, 4, 2, 64], BF16)
        vT_ = sb("vT_", [128, 4, 2, 64], BF16)
        ks = sb("ks", [64, 4, 256], BF16)
        v1 = sb("v1", [64, 4, 257], BF16)
        sigoT = sb("sigoT", [128, 2, 64], BF16)
        gT = sb("gT", [64, 8])
        nlf = sb("nlf", [64, 4])
        g_T = sb("g_T", [64, 4])
        nb_T = sb("nb_T", [64, 4])
        cm = [sb("cm%d" % i, [4, 64]) for i in range(2)]
        nbR = sb("nbR", [4, 64])
        MR = sb("MR", [4, 64])
        RW = sb("RW", [4, 2, 64])
        tmpR = sb("tmpR", [4, 64])
        mnew = sb("mnew", [4, 1])
        cols_T = sb("cols_T", [64, 8])
        wli = sb("wli", [128, 4])
        wT = sb("wT", [64, 64])
        SwT = sb("SwT", [64, 64], BF16)
        tmpA = sb("tmpA", [64, 257])
        nd = sb("nd", [64, 257])
        dn = sb("dn", [64, 1])
        hh = sb("hh", [64, 256])
        junk = sb("junk", [128, 256], BF16)
        ssq = sb("ssq", [128, 4])
        junk2 = sb("junk2", [128, 256], BF16)
        ssq2 = sb("ssq2", [128, 4])
        ssq3 = sb("ssq3", [128, 4])
        hn = sb("hn", [64, 256], BF16)
        wlv = sb("wlv", [64, 257], BF16)
        dt_T = sb("dt_T", [128, 32])
        a_T = sb("a_T", [128, 32])
        cs_T = sb("cs_T", [128, 32])
        ncs_T = sb("ncs_T", [128, 32])
        expcs_T = sb("expcs_T", [128, 32])
        wend = sb("wend", [128, 32])
        declast = sb("declast", [128, 32])
        B_T = sb("B_T", [128, 512], BF16)
        xs_T = sb("xs_T", [128, 512])
        xdt = sb("xdt", [128, 512], BF16)
        wx = sb("wx", [128, 512], BF16)
        zs_T = sb("zs_T", [128, 512])
        CBs = sb("CBs", [128, 128])
        A1h = sb("A1h", [128, 512])
        decTh = sb("decTh", [128, 512])
        MTh = sb("MTh", [128, 512], BF16)
        yt = sb("yt", [128, 512])
        yt2 = sb("yt2", [128, 512])
        yo = sb("yo", [128, 512], BF16)
        vn = sb("vn", [128, 1024])
        rowstg = vn
        vnb = sb("vnb", [128, 1024], BF16)
        tmpc = sb("tmpc", [128, 128])

        c.op("dve", lambda e: e.memset(v1.t[:, :, 256:257], 1.0), writes=[v1.b])

        def load_w(src_ap, ktiles, width, scr):
            rot["w"] ^= 1
            slot = wslot[rot["w"]]
            n = ktiles * width
            view = slot.t[:, 0:n].rearrange("p (t c) -> p t c", c=width)
            if phase["p"] == "prompt":
                dma("sp", slot.t[:, 0:n], scr[:, 0:n], slot, W=[slot.b])
                return slot, view
            srcv = src_ap.rearrange("(t p) c -> p t c", p=128)
            for t0 in range(0, ktiles, WSPLIT):
                dma("pool", view[:, t0:t0 + WSPLIT, :], srcv[:, t0:t0 + WSPLIT, :], slot, W=[slot.b])
            if phase["p"] == "sample":
                dma("sp", scr[:, 0:n], slot.t[:, 0:n], slot, R=[slot.b])
            return slot, view

        def rmsnorm_stats(NT, src):
            p = pss()
            for i in range(NDT):
                s_ = sq[i % 2]
                act(s_.t[:, 0:NT], src.t[:, i, 0:NT], AF.Square, [src.b], [s_.b])
                mm(p, p.t[:, 0:NT], ONES, s_.t[:, 0:NT], i == 0, i == NDT - 1, [CS.b, s_.b])
            act(rstd_bc.t[:, 0:NT], p.t[:, 0:NT], AF.Ln, [p.b], [rstd_bc.b], scale=1.0 / D, bias=EPS)
            act(rstd_bc.t[:, 0:NT], rstd_bc.t[:, 0:NT], AF.Exp, [rstd_bc.b], [rstd_bc.b], scale=-0.5)

        def conv_views(buf, i, nseq, T):
            return buf.t[:, i, 0:nseq * (3 + T)].rearrange("p (j w) -> p j w", w=3 + T)

        def tokview(ap2d, nseq, T):
            return ap2d.rearrange("p (j t) -> p j t", t=T)

        GI0 = {}
        _g = 0
        for (nm_, c0_, n_) in SEGS:
            GI0[nm_] = _g
            _g += (n_ + GW - 1) // GW

        def in_proj(l, NT, nseq, T, segs=None):
            for (nm, c0, ncols) in SEGS:
                if segs is not None and nm not in segs:
                    continue
                gi = GI0[nm] - 1
                g0 = 0
                while g0 < ncols:
                    gw = min(GW, ncols - g0)
                    gi += 1
                    slot, wv = load_w(w_in[l][:, c0 + g0:c0 + g0 + gw], NDT, gw, wscr_in[l, gi])
                    for t0 in range(0, gw, 128):
                        tw = min(128, gw - t0)
                        i = (g0 + t0) // 128
                        p = pss()
                        for k in range(NDT):
                            mm(p, p.t[0:tw, 0:NT], wv[:, k, t0:t0 + tw], hT.t[:, k, 0:NT], k == 0, k == NDT - 1,
                               [slot.b, hT.b])
                        pin = p.t[0:tw, 0:NT]
                        if nm == "xm":
                            cp(conv_views(xmin, i, nseq, T)[:, :, 3:3 + T], tokview(pin, nseq, T), [p.b], [xmin.b])
                            cp(xmT.t[:, i, 0:NT], pin, [p.b], [xmT.b])
                        elif nm == "zm":
                            act(zmS.t[:, i, 0:NT], pin, AF.Silu, [p.b], [zmS.b])
                        elif nm == "zs":
                            act(zsS.t[:, i, 0:NT], pin, AF.Silu, [p.b], [zsS.b])
                        elif nm == "xbc":
                            cp(conv_views(sbcin, i, nseq, T)[:, :, 3:3 + T], tokview(pin, nseq, T), [p.b], [sbcin.b])
                        elif nm == "dt":
                            cp(dtR.t[0:32, 0:NT], pin, [p.b], [dtR.b])
                        elif nm == "u":
                            cp(uT.t[:, i, 0:NT], pin, [p.b], [uT.b])
                        elif nm == "v":
                            cp(vT.t[:, i, 0:NT], pin, [p.b], [vT.b])
                        elif nm == "zc":
                            act(zcS.t[:, i, 0:NT], pin, AF.Silu, [p.b], [zcS.b])
                    g0 += gw
                ck("seg_" + nm)

        def conv(l, NT, nseq, T, which=("xm", "xbc")):
            for (buf, wts, ntile) in ((xmin, cwm, 8), (sbcin, cws, 24)):
                if ("xm" if buf is xmin else "xbc") not in which:
                    continue
                for i in range(ntile):
                    xin = conv_views(buf, i, nseq, T)
                    acc = tokview(cacc.t[:, 0:NT], nseq, T)
                    ts(acc, xin[:, :, 0:T], wts.t[:, l, i, 0:1], wts.t[:, l, i, 4:5], ALU.mult, ALU.add,
                       [buf.b, wts.b], [cacc.b])
                    for k in range(1, 4):
                        stt(acc, xin[:, :, k:k + T], wts.t[:, l, i, k:k + 1], acc, ALU.mult, ALU.add,
                            [buf.b, wts.b, cacc.b], [cacc.b])
                    if buf is xmin:
                        dst = xmcT
                        o = xmcT.t[:, i, 0:NT]
                    elif i < 16:
                        dst = xsT_
                        o = xsT_.t[:, i, 0:NT]
                    elif i < 20:
                        dst = BT
                        o = BT.t[:, i - 16, 0:NT]
                    else:
                        dst = CTT
                        o = CTT.t[:, i - 20, 0:NT]
                    act(o, cacc.t[:, 0:NT], AF.Silu, [cacc.b], [dst.b])

        def conv_state_out(l, NT, kind):
            for (buf, ntile, dstp, dsts) in ((xmin, 8, pmc, omc), (sbcin, 24, psc, osc)):
                for ch in range(ntile // 8):
                    for half in range(2):
                        p = psb()
                        for j in range(4):
                            i = ch * 8 + half * 4 + j
                            if kind == "p":
                                src = buf.t[:, i, 3:3 + NT]
                            else:
                                src = buf.t[:, i, 0:NSEQ * 7].rearrange("p (j w) -> p j w", w=7)[:, :, 3:7]
                            if kind == "p":
                                mm(p, p.t[0:NT, j * 128:(j + 1) * 128], src, I_f, True, True, [buf.b, CS.b])
                            else:
                                cp(cacc.t[:, 0:NT].rearrange("p (j t) -> p j t", t=TS), src, [buf.b], [cacc.b])
                                mm(p, p.t[0:NT, j * 128:(j + 1) * 128], cacc.t[:, 0:NT], I_f, True, True,
                                   [cacc.b, CS.b])
                        cp(rowstg.t[0:NT, half * 512:(half + 1) * 512], p.t[0:NT, 0:512], [p.b], [rowstg.b])
                    cc = slice(ch * 1024, (ch + 1) * 1024)
                    if kind == "p":
                        dma("sp", dstp[l][:, cc], rowstg.t[NT - 3:NT, :], rowstg, R=[rowstg.b])
                    else:
                        for j in range(NSEQ):
                            dma("sp", dsts[l, j][:, cc], rowstg.t[j * 4 + 1:j * 4 + 4, :], rowstg, R=[rowstg.b])

        def mlstm_chunk(l, c0, L, s_):
            cs = slice(c0, c0 + L)
            for h in range(4):
                for (src, W, e0, dst) in ((xmcT, wqk, 0, qT), (xmcT, wqk, 256, kT), (xmT, wvo, 0, vT_)):
                    for et in range(2):
                        p = pss()
                        for dt in range(2):
                            mm(p, p.t[:, 0:L], W.t[:, h * 2 + dt, e0 + et * 128:e0 + (et + 1) * 128],
                               src.t[:, h * 2 + dt, cs], dt == 0, dt == 1, [W.b, src.b])
                        cp(dst.t[:, h, et, 0:L], p.t[:, 0:L], [p.b], [dst.b])
                p = pss()
                for dt in range(2):
                    mm(p, p.t[0:L, 0:256], xmcT.t[:, h * 2 + dt, cs], wqk.t[:, h * 2 + dt, 256:512], dt == 0, dt == 1,
                       [xmcT.b, wqk.b])
                act(ks.t[0:L, h, :], p.t[0:L, 0:256], AF.Copy, [p.b], [ks.b], scale=0.0625)
                p = pss()
                for dt in range(2):
                    mm(p, p.t[0:L, 0:256], xmT.t[:, h * 2 + dt, cs], wvo.t[:, h * 2 + dt, 0:256], dt == 0, dt == 1,
                       [xmT.b, wvo.b])
                cp(v1.t[0:L, h, 0:256], p.t[0:L, 0:256], [p.b], [v1.b])
                yield
            pg = pss()
            n = 0
            for h in range(4):
                for part, srcT in enumerate((qT, kT, vT_)):
                    for dt in range(2):
                        mm(pg, pg.t[0:L, 0:8], srcT.t[:, h, dt, 0:L], wg.t[:, l, (h * 3 + part) * 2 + dt, :],
                           n == 0, n == 23, [srcT.b, wg.b])
                        n += 1
            tt(gT.t[0:L, :], pg.t[0:L, 0:8], bg_bc.t[0:L, l, :], ALU.add, [pg.b, bg_bc.b], [gT.b])
            act(nlf.t[0:L, :], gT.t[0:L, 4:8], AF.Exp, [gT.b], [nlf.b], scale=-1.0)
            act(nlf.t[0:L, :], nlf.t[0:L, :], AF.Ln, [nlf.b], [nlf.b], bias=1.0)
            p = pss()
            mm(p, p.t[0:L, 0:4], U_f[0:L, 0:L], nlf.t[0:L, :], True, True, [CS.b, nlf.b])
            tt(g_T.t[0:L, :], p.t[0:L, 0:4], gT.t[0:L, 0:4], ALU.add, [p.b, gT.b], [g_T.b])
            cp(nb_T.t[0:L, :], p.t[0:L, 0:4], [p.b], [nb_T.b])
            p2 = pss()
            mm(p2, p2.t[0:4, 0:L], g_T.t[0:L, :], I_f[0:L, 0:L], True, True, [g_T.b, CS.b])
            mm(p2, p2.t[0:4, 64:64 + L], nb_T.t[0:L, :], I_f[0:L, 0:L], True, True, [nb_T.b, CS.b])
            cp(cm[0].t[:, 0:L], p2.t[0:4, 0:L], [p2.b], [cm[0].b])
            cp(nbR.t[:, 0:L], p2.t[0:4, 64:64 + L], [p2.b], [nbR.b])
            src = 0
            shf = 1
            while shf < L:
                dst = 1 - src
                cp(cm[dst].t[:, 0:shf], cm[src].t[:, 0:shf], [cm[src].b], [cm[dst].b])
                tt(cm[dst].t[:, shf:L], cm[src].t[:, shf:L], cm[src].t[:, 0:L - shf], ALU.max, [cm[src].b], [cm[dst].b])
                src = dst
                shf *= 2
            ts(MR.t[:, 0:L], cm[src].t[:, 0:L], s_.m.t[:, 0:1], None, ALU.max, None, [cm[src].b, s_.m.b], [MR.b])
            act(RW.t[:, 0, 0:L], MR.t[:, 0:L], AF.Exp, [MR.b, s_.m.b], [RW.b], scale=-1.0, bias=s_.m.t[:, 0:1])
            tt(tmpR.t[:, 0:L], nbR.t[:, 0:L], MR.t[:, 0:L], ALU.subtract, [nbR.b, MR.b], [tmpR.b])
            act(RW.t[:, 1, 0:L], tmpR.t[:, 0:L], AF.Exp, [tmpR.b], [RW.b])
            tt(mnew.t[:, 0:1], MR.t[:, L - 1:L], nbR.t[:, L - 1:L], ALU.subtract, [MR.b, nbR.b], [mnew.b])
            p3 = pss()
            mm(p3, p3.t[0:L, 0:4], RW.t[:, 0, 0:L], I_f[0:4, 0:4], True, True, [RW.b, CS.b])
            mm(p3, p3.t[0:L, 4:8], RW.t[:, 1, 0:L], I_f[0:4, 0:4], True, True, [RW.b, CS.b])
            cp(cols_T.t[0:L, :], p3.t[0:L, 0:8], [p3.b], [cols_T.b])
            p4 = pss()
            mm(p4, p4.t[:, 0:4], SELL[L][0:L, :], cols_T.t[0:L, 0:4], True, True, [CS.b, cols_T.b])
            cp(wli.t[:, :], p4.t[:, 0:4], [p4.b], [wli.b])
            act(CTb.t[:], s_.CT.t[:], AF.Copy, [s_.CT.b], [CTb.b])
            yield
            for h in range(4):
                p5 = pss()
                mm(p5, p5.t[0:L, 0:L], SEL4[h][0:4, 0:L], MR.t[:, 0:L], True, False, [CS.b, MR.b])
                mm(p5, p5.t[0:L, 0:L], I_f[0:L, 0:L], MPOS[0:L, 0:L], False, True, [CS.b])
                act(wT.t[0:L, 0:L], p5.t[0:L, 0:L], AF.Exp, [p5.b, g_T.b], [wT.b], scale=-1.0, bias=g_T.t[0:L, h:h + 1])
                p6 = pss()
                for dt in range(2):
                    mm(p6, p6.t[0:L, 0:L], kT.t[:, h, dt, 0:L], qT.t[:, h, dt, 0:L], dt == 0, dt == 1, [kT.b, qT.b])
                stt(SwT.t[0:L, 0:L], p6.t[0:L, 0:L], 0.0625, wT.t[0:L, 0:L], ALU.mult, ALU.mult, [p6.b, wT.b], [SwT.b])
                pa = psb()
                mm(pa, pa.t[0:L, 0:257], SwT.t[0:L, 0:L], v1.t[0:L, h, :], True, True, [SwT.b, v1.b])
                pb = psb()
                for kt in range(2):
                    mm(pb, pb.t[0:L, 0:257], qT.t[:, h, kt, 0:L], CTb.t[:, h, kt, :], kt == 0, kt == 1, [qT.b, CTb.b])
                act(tmpA.t[0:L, :], pa.t[0:L, 0:257], AF.Copy, [pa.b], [tmpA.b])
                stt(nd.t[0:L, :], pb.t[0:L, 0:257], cols_T.t[0:L, h:h + 1], tmpA.t[0:L, :], ALU.mult, ALU.add,
                    [pb.b, cols_T.b, tmpA.b], [nd.b])
                yield
                act(dn.t[0:L, :], nd.t[0:L, 256:257], AF.Abs, [nd.b], [dn.b])
                tt(dn.t[0:L, :], dn.t[0:L, :], cols_T.t[0:L, 4 + h:5 + h], ALU.max, [dn.b, cols_T.b], [dn.b])
                c.op("dve", lambda e: e.reciprocal(dn.t[0:L, :], dn.t[0:L, :]), reads=[dn.b], writes=[dn.b])
                for et in range(2):
                    p = pss()
                    for dt in range(2):
                        mm(p, p.t[:, 0:L], wvo.t[:, h * 2 + dt, 256 + et * 128:256 + (et + 1) * 128], xmT.t[:, h * 2 + dt, cs],
                           dt == 0, dt == 1, [wvo.b, xmT.b])
                    act(sigoT.t[:, et, 0:L], p.t[:, 0:L], AF.Sigmoid, [p.b, mbo.b], [sigoT.b],
                        bias=mbo.t[:, l, h * 2 + et:h * 2 + et + 1])
                po = pss()
                for et in range(2):
                    mm(po, po.t[0:L, et * 128:(et + 1) * 128], sigoT.t[:, et, 0:L], I_b.t[:, :], True, True, [sigoT.b, I_b.b])
                stt(hh.t[0:L, :], nd.t[0:L, 0:256], dn.t[0:L, 0:1], po.t[0:L, 0:256], ALU.mult, ALU.mult,
                    [nd.b, dn.b, po.b], [hh.b])
                act(junk.t[0:L, 0:256], hh.t[0:L, :], AF.Square, [hh.b], [junk.b, ssq.b], accum_out=ssq.t[0:L, 0:1])
                act(ssq.t[0:L, 0:1], ssq.t[0:L, 0:1], AF.Ln, [ssq.b], [ssq.b], scale=1.0 / 256, bias=EPS)
                act(ssq.t[0:L, 0:1], ssq.t[0:L, 0:1], AF.Exp, [ssq.b], [ssq.b], scale=-0.5)
                ts(hn.t[0:L, :], hh.t[0:L, :], ssq.t[0:L, 0:1], None, ALU.mult, None, [hh.b, ssq.b], [hn.b])
                for et in range(2):
                    p7 = pss()
                    mm(p7, p7.t[:, 0:L], hn.t[0:L, et * 128:(et + 1) * 128], I_b.t[0:L, 0:L], True, True, [hn.b, I_b.b])
                    stt(mixT.t[:, h * 2 + et, cs], p7.t[:, 0:L], mng.t[:, l, h * 2 + et:h * 2 + et + 1],
                        zmS.t[:, h * 2 + et, cs], ALU.mult, ALU.mult, [p7.b, mng.b, zmS.b], [mixT.b])
                yield
                ts(wlv.t[0:L, :], v1.t[0:L, h, :], wT.t[0:L, L - 1:L], None, ALU.mult, None, [v1.b, wT.b], [wlv.b])
                for kt in range(2):
                    p8 = psb()
                    mm(p8, p8.t[:, 0:257], ks.t[0:L, h, kt * 128:(kt + 1) * 128], wlv.t[0:L, :], True, True, [ks.b, wlv.b])
                    stt(s_.CT.t[:, h, kt, :], s_.CT.t[:, h, kt, :], wli.t[:, h:h + 1], p8.t[:, 0:257], ALU.mult, ALU.add,
                        [s_.CT.b, wli.b, p8.b], [s_.CT.b])
                yield
            cp(s_.m.t[:, :], mnew.t[:, :], [mnew.b], [s_.m.b])

        def ssd_chunk(l, c0, L, s_):
            cs = slice(c0, c0 + L)
            p = pss()
            mm(p, p.t[0:L, 0:32], dtR.t[0:32, cs], I_f[0:32, 0:32], True, True, [dtR.b, CS.b])
            tt(dt_T.t[0:L, :], p.t[0:L, 0:32], dtb_bc.t[0:L, l, :], ALU.add, [p.b, dtb_bc.b], [dt_T.b])
            act(dt_T.t[0:L, :], dt_T.t[0:L, :], AF.Exp, [dt_T.b], [dt_T.b])
            act(dt_T.t[0:L, :], dt_T.t[0:L, :], AF.Ln, [dt_T.b], [dt_T.b], bias=1.0)
            tt(a_T.t[0:L, :], dt_T.t[0:L, :], A_bc.t[0:L, l, :], ALU.mult, [dt_T.b, A_bc.b], [a_T.b])
            p = pss()
            mm(p, p.t[0:L, 0:32], U_f[0:L, 0:L], a_T.t[0:L, :], True, True, [CS.b, a_T.b])
            act(cs_T.t[0:L, :], p.t[0:L, 0:32], AF.Copy, [p.b], [cs_T.b])
            act(expcs_T.t[0:L, :], p.t[0:L, 0:32], AF.Exp, [p.b], [expcs_T.b])
            ts(ncs_T.t[0:L, :], p.t[0:L, 0:32], -1.0, None, ALU.mult, None, [p.b], [ncs_T.b])
            p2 = pss()
            mm(p2, p2.t[:, 0:32], SELL[L][0:L, :], cs_T.t[0:L, :], True, True, [CS.b, cs_T.b])
            act(declast.t[:, :], p2.t[:, 0:32], AF.Exp, [p2.b], [declast.b])
            tt(wend.t[0:L, :], p2.t[0:L, 0:32], cs_T.t[0:L, :], ALU.subtract, [p2.b, cs_T.b], [wend.b])
            act(wend.t[0:L, :], wend.t[0:L, :], AF.Exp, [wend.b], [wend.b])
            tt(wend.t[0:L, :], wend.t[0:L, :], dt_T.t[0:L, :], ALU.mult, [wend.b, dt_T.b], [wend.b])
            p3 = psb()
            for g in range(4):
                mm(p3, p3.t[0:L, g * 128:(g + 1) * 128], BT.t[:, g, cs], I_b.t[:, :], True, True, [BT.b, I_b.b])
            cp(B_T.t[0:L, :], p3.t[0:L, 0:512], [p3.b], [B_T.b])
            yield
            for g in range(4):
                hs = slice(g * 8, (g + 1) * 8)
                act(HTb.t[:], s_.HT.t[:, hs, :], AF.Copy, [s_.HT.b], [HTb.b])
                px = psb()
                for j in range(4):
                    mm(px, px.t[0:L, j * 128:(j + 1) * 128], xsT_.t[:, g * 4 + j, cs], I_f, True, True, [xsT_.b, CS.b])
                act(xs_T.t[0:L, :], px.t[0:L, 0:512], AF.Copy, [px.b], [xs_T.b])
                xv = xs_T.t[0:L, :].rearrange("p (h d) -> p h d", d=64)
                tt(xdt.t[0:L, :].rearrange("p (h d) -> p h d", d=64), xv,
                   dt_T.t[0:L, hs].unsqueeze(2).to_broadcast([L, 8, 64]), ALU.mult, [xs_T.b, dt_T.b], [xdt.b])
                tt(wx.t[0:L, :].rearrange("p (h d) -> p h d", d=64), xv,
                   wend.t[0:L, hs].unsqueeze(2).to_broadcast([L, 8, 64]), ALU.mult, [xs_T.b, wend.b], [wx.b])
                pz = psb()
                for j in range(4):
                    mm(pz, pz.t[0:L, j * 128:(j + 1) * 128], zsS.t[:, g * 4 + j, cs], I_f, True, True, [zsS.b, CS.b])
                act(zs_T.t[0:L, :], pz.t[0:L, 0:512], AF.Copy, [pz.b], [zs_T.b])
                pc = pss()
                mm(pc, pc.t[0:L, 0:L], BT.t[:, g, cs], CTT.t[:, g, cs], True, True, [BT.b, CTT.b])
                act(CBs.t[0:L, 0:L], pc.t[0:L, 0:L], AF.Copy, [pc.b], [CBs.b])
                pi = psb()
                mm(pi, pi.t[0:L, 0:512], CTT.t[:, g, cs], HTb.t[:].rearrange("p h d -> p (h d)"), True, True,
                   [CTT.b, HTb.b])
                py = psb()
                yield
                tt(CBs.t[0:L, 0:L], CBs.t[0:L, 0:L], U_f[0:L, 0:L], ALU.mult, [CBs.b, CS.b], [CBs.b])
                for hf in range(2):
                    h0 = g * 8 + hf * 4
                    n4 = 4 * L
                    a3 = A1h.t[0:L, 0:n4].rearrange("p (j t) -> p j t", t=L)
                    tt(a3, U_f[0:L, 0:L].unsqueeze(1).to_broadcast([L, 4, L]),
                       a_T.t[0:L, h0:h0 + 4].unsqueeze(2).to_broadcast([L, 4, L]), ALU.mult, [CS.b, a_T.b], [A1h.b])
                    pd = pss()
                    mm(pd, pd.t[0:L, 0:n4], ONES[0:L, 0:L], A1h.t[0:L, 0:n4], True, True, [CS.b, A1h.b])
                    for j4 in range(4):
                        ts(decTh.t[0:L, j4 * L:(j4 + 1) * L], pd.t[0:L, j4 * L:(j4 + 1) * L],
                           ncs_T.t[0:L, h0 + j4:h0 + j4 + 1], 0.0, ALU.add, ALU.min, [pd.b, ncs_T.b], [decTh.b])
                    act(decTh.t[0:L, 0:n4], decTh.t[0:L, 0:n4], AF.Exp, [decTh.b], [decTh.b])
                    tt(MTh.t[0:L, 0:n4].rearrange("p (j t) -> p j t", t=L),
                       decTh.t[0:L, 0:n4].rearrange("p (j t) -> p j t", t=L),
                       CBs.t[0:L, 0:L].unsqueeze(1).to_broadcast([L, 4, L]), ALU.mult, [decTh.b, CBs.b], [MTh.b])
                    for j4 in range(4):
                        hh_ = hf * 4 + j4
                        mm(py, py.t[0:L, hh_ * 64:(hh_ + 1) * 64], MTh.t[0:L, j4 * L:(j4 + 1) * L],
                           xdt.t[0:L, hh_ * 64:(hh_ + 1) * 64], True, True, [MTh.b, xdt.b])
                    yield
                y3 = yt.t[0:L, :].rearrange("p (h d) -> p h d", d=64)
                tt(y3, pi.t[0:L, 0:512].rearrange("p (h d) -> p h d", d=64),
                   expcs_T.t[0:L, hs].unsqueeze(2).to_broadcast([L, 8, 64]), ALU.mult, [pi.b, expcs_T.b], [yt.b])
                tt(yt.t[0:L, :], yt.t[0:L, :], py.t[0:L, 0:512], ALU.add, [yt.b, py.b], [yt.b])
                tt(yt2.t[0:L, :].rearrange("p (h d) -> p h d", d=64), xv,
                   D_bc.t[0:L, l, hs].unsqueeze(2).to_broadcast([L, 8, 64]), ALU.mult, [xs_T.b, D_bc.b], [yt2.b])
                tt(yt.t[0:L, :], yt.t[0:L, :], yt2.t[0:L, :], ALU.add, [yt.b, yt2.b], [yt.b])
                tt(yt.t[0:L, :], yt.t[0:L, :], zs_T.t[0:L, :], ALU.mult, [yt.b, zs_T.b], [yt.b])
                act(yt2.t[0:L, 0:512], yt.t[0:L, :], AF.Square, [yt.b], [yt2.b, ssq3.b], accum_out=ssq3.t[0:L, 1:2])
                act(ssq3.t[0:L, 1:2], ssq3.t[0:L, 1:2], AF.Ln, [ssq3.b], [ssq3.b], scale=1.0 / 512, bias=EPS)
                act(ssq3.t[0:L, 1:2], ssq3.t[0:L, 1:2], AF.Exp, [ssq3.b], [ssq3.b], scale=-0.5)
                ts(yo.t[0:L, :], yt.t[0:L, :], ssq3.t[0:L, 1:2], None, ALU.mult, None, [yt.b, ssq3.b], [yo.b])
                yield
                for j in range(4):
                    p9 = pss()
                    mm(p9, p9.t[:, 0:L], yo.t[0:L, j * 128:(j + 1) * 128], I_b.t[0:L, 0:L], True, True, [yo.b, I_b.b])
                    ts(mixT.t[:, 8 + g * 4 + j, cs], p9.t[:, 0:L], sng.t[:, l, g * 4 + j:g * 4 + j + 1], None, ALU.mult, None,
                       [p9.b, sng.b], [mixT.b])
                pst = psb()
                mm(pst, pst.t[:, 0:512], B_T.t[0:L, g * 128:(g + 1) * 128], wx.t[0:L, :], True, True, [B_T.b, wx.b])
                tt(s_.HT.t[:, hs, :], s_.HT.t[:, hs, :], declast.t[:, hs].unsqueeze(2).to_broadcast([128, 8, 64]), ALU.mult,
                   [s_.HT.b, declast.b], [s_.HT.b])
                tt(s_.HT.t[:, hs, :], s_.HT.t[:, hs, :], pst.t[:, 0:512].rearrange("p (h d) -> p h d", d=64), ALU.add,
                   [s_.HT.b, pst.b], [s_.HT.b])

        def cmlp_chunk(l, c0, L, cv_dst=None):
            cs = slice(c0, c0 + L)
            for hf in range(2):
                p = psb()
                for i4 in range(4):
                    i = hf * 4 + i4
                    mm(p, p.t[0:L, i4 * 128:(i4 + 1) * 128], vT.t[:, i, cs], I_f, True, True, [vT.b, CS.b])
                for g2 in range(2):
                    g = hf * 2 + g2
                    act(junk2.t[0:L, 0:256], p.t[0:L, g2 * 256:(g2 + 1) * 256], AF.Square, [p.b], [junk2.b, ssq2.b],
                        accum_out=ssq2.t[0:L, g:g + 1])
                act(ssq2.t[0:L, hf * 2:hf * 2 + 2], ssq2.t[0:L, hf * 2:hf * 2 + 2], AF.Ln, [ssq2.b], [ssq2.b],
                    scale=1.0 / 256, bias=EPS)
                act(ssq2.t[0:L, hf * 2:hf * 2 + 2], ssq2.t[0:L, hf * 2:hf * 2 + 2], AF.Exp, [ssq2.b], [ssq2.b], scale=-0.5)
                tt(vn.t[0:L, hf * 512:(hf + 1) * 512].rearrange("p (g d) -> p g d", d=256),
                   p.t[0:L, 0:512].rearrange("p (g d) -> p g d", d=256),
                   ssq2.t[0:L, hf * 2:hf * 2 + 2].unsqueeze(2).to_broadcast([L, 2, 256]), ALU.mult, [p.b, ssq2.b], [vn.b])
                yield
            tt(vn.t[0:L, :], vn.t[0:L, :], cvg_bc.t[0:L, :], ALU.mult, [vn.b, cvg_bc.b], [vn.b])
            if cv_dst is not None:
                dma("sp", cv_dst, vn.t[0:L, :], vn, R=[vn.b])
            act(vnb.t[0:L, :], vn.t[0:L, :], AF.Copy, [vn.b], [vnb.b])
            yield
            for g in range(4):
                yield
                for dt in range(2):
                    p = pss()
                    mm(p, p.t[:, 0:L], vnb.t[0:L, g * 256 + dt * 128:g * 256 + (dt + 1) * 128], WmT.t[0:L, l, g, 0:L],
                       True, True, [vnb.b, WmT.b])
                    tt(tmpc.t[:, 0:L], p.t[:, 0:L], cb_bc.t[:, l, g, 0:L], ALU.add, [p.b, cb_bc.b], [tmpc.b])
                    tt(tmpc.t[:, 0:L], tmpc.t[:, 0:L], uT.t[:, g * 2 + dt, cs], ALU.mult, [tmpc.b, uT.b], [tmpc.b])
                    tt(mixT.t[:, 24 + g * 2 + dt, cs], tmpc.t[:, 0:L], zcS.t[:, g * 2 + dt, cs], ALU.mult,
                       [tmpc.b, zcS.b], [mixT.b])

        def run_il(chains):
            recs = []
            for (pname, mk) in chains:
                c.rec = []
                cpool["p"] = CH_POOLS[pname]
                r_ = mk()
                if r_ is not None:
                    for _ in r_:
                        pass
                recs.append(c.rec)
                c.rec = None
                cpool["p"] = None
            pos = [0] * len(recs)
            total = sum(len(r) for r in recs)
            for _ in range(total):
                best, bf = None, 2.0
                for i, r in enumerate(recs):
                    if pos[i] < len(r):
                        fr = pos[i] / len(r)
                        if fr < bf:
                            best, bf = i, fr
                kind, e, fn, R, W, inc_ = recs[best][pos[best]]
                pos[best] += 1
                if kind == "op":
                    c.op(e, fn, reads=R, writes=W, inc=inc_)
                else:
                    out, in_, sbuf_, kw = fn
                    c.dma(e, out, in_, sbuf_, reads=R, writes=W, **kw)

        def mlstm_seq(l, chunks, L, s_):
            for c0 in chunks:
                yield from mlstm_chunk(l, c0, L, s_)

        def out_proj(l, NT):
            for dtile in range(NDT):
                slot, wv = load_w(w_out[l][:, dtile * 128:(dtile + 1) * 128], 32, 128, wscr_out[l, dtile])
                p = pss()
                for e_ in range(32):
                    mm(p, p.t[:, 0:NT], wv[:, e_, :], mixT.t[:, e_, 0:NT], e_ == 0, e_ == 31, [slot.b, mixT.b])
                tt(xres.t[:, dtile, 0:NT], xres.t[:, dtile, 0:NT], p.t[:, 0:NT], ALU.add, [xres.b, p.b], [xres.b])

        def load_qkvo(l):
            dma("sp", cvg_bc.t[:], c_v_norm_g[l].partition_broadcast(128), cvg_bc, W=[cvg_bc.b])
            dma("pool", wqk.t[:], m_w_qk[l].rearrange("h (dt p) e -> p (h dt) e", p=128), wqk, W=[wqk.b])
            dma("pool", wvo.t[:], m_w_vo[l].rearrange("h (dt p) e -> p (h dt) e", p=128), wvo, W=[wvo.b])

        def norm_to_hT(l, NT):
            rmsnorm_stats(NT, xres)
            for i in range(NDT):
                stt(hT.t[:, i, 0:NT], xres.t[:, i, 0:NT], gcol.t[:, l, i:i + 1], rstd_bc.t[:, 0:NT], ALU.mult, ALU.mult,
                    [xres.b, gcol.b, rstd_bc.b], [hT.b])

        def final_out(NT, dst):
            rmsnorm_stats(NT, xres)
            dv = dst.rearrange("(t p) c -> p t c", p=128)
            for hf, buf in enumerate((uT, vT)):
                for i in range(8):
                    k = hf * 8 + i
                    stt(buf.t[:, i, 0:NT], xres.t[:, k, 0:NT], gcol.t[:, 2, k:k + 1], rstd_bc.t[:, 0:NT], ALU.mult, ALU.mult,
                        [xres.b, gcol.b, rstd_bc.b], [buf.b])
                dma("sp", dv[:, hf * 8:(hf + 1) * 8, :], buf.t[:, :, 0:NT], buf, R=[buf.b])

        def load_state(s_, srcC, srcH, j, stg):
            dma("sp", stg.t[:].rearrange("p (a k) -> p a k", k=256), srcC.rearrange("h (vt p) k -> p (h vt) k", p=128),
                stg, W=[stg.b])
            for h in range(4):
                p = psb()
                for kt in range(2):
                    for vt in range(2):
                        mm(p, p.t[:, kt * 256 + vt * 128:kt * 256 + (vt + 1) * 128],
                           stg.t[:, (h * 2 + vt) * 256 + kt * 128:(h * 2 + vt) * 256 + (kt + 1) * 128], I_f, True, True,
                           [stg.b, CS.b])
                cp(s_.CT.t[:, h, :, 0:256], p.t[:, 0:512].rearrange("p (a v) -> p a v", v=256), [p.b], [s_.CT.b])
            cp(s_.CT.t[:, :, :, 256], nT_all.t[:, :, j * 4:(j + 1) * 4].rearrange("p kt h -> p h kt"), [nT_all.b], [s_.CT.b])
            cp(s_.m.t[:, 0:1], m_all.t[:, j:j + 1], [m_all.b], [s_.m.b])
            dma("sp", stg.t[:].rearrange("p (a n) -> p a n", n=128), srcH.rearrange("(pr hh) p n -> (hh p) pr n", hh=2),
                stg, W=[stg.b])
            for q4 in range(4):
                p = psb()
                for j2 in range(4):
                    pr = q4 * 4 + j2
                    mm(p, p.t[:, j2 * 128:(j2 + 1) * 128], stg.t[:, pr * 128:(pr + 1) * 128], I_f, True, True, [stg.b, CS.b])
                cp(s_.HT.t[:, q4 * 8:(q4 + 1) * 8, :], p.t[:, 0:512].rearrange("p (h d) -> p h d", d=64), [p.b], [s_.HT.b])

        def store_state(s_, dstC, dstn, dstm, dstH, stg):
            for h in range(4):
                p = psb()
                for vt in range(2):
                    for kt in range(2):
                        mm(p, p.t[:, vt * 256 + kt * 128:vt * 256 + (kt + 1) * 128],
                           s_.CT.t[:, h, kt, vt * 128:(vt + 1) * 128], I_f, True, True, [s_.CT.b, CS.b])
                cp(stg.t[:, h * 512:(h + 1) * 512], p.t[:, 0:512], [p.b], [stg.b])
            dma("sp", dstC.rearrange("h (vt p) k -> p (h vt) k", p=128), stg.t[:].rearrange("p (a k) -> p a k", k=256), stg,
                R=[stg.b])
            p = pss()
            for kt in range(2):
                mm(p, p.t[0:4, kt * 128:(kt + 1) * 128], s_.CT.t[:, :, kt, 256], I_f, True, True, [s_.CT.b, CS.b])
            cp(nrow.t[0:4, :], p.t[0:4, 0:256], [p.b], [nrow.b])
            dma("sp", dstn, nrow.t[0:4, :], nrow, R=[nrow.b])
            dma("sp", dstm.rearrange("(h o) -> h o", o=1), s_.m.t[:, 0:1], s_.m, R=[s_.m.b])
            for q4 in range(4):
                p = psb()
                for j2 in range(4):
                    pr = q4 * 4 + j2
                    mm(p, p.t[:, j2 * 128:(j2 + 1) * 128], s_.HT.t[:, pr * 2:pr * 2 + 2, :].rearrange("p h d -> p (h d)"), I_f,
                       True, True, [s_.HT.b, CS.b])
                cp(stg.t[:, q4 * 512:(q4 + 1) * 512], p.t[:, 0:512], [p.b], [stg.b])
            dma("sp", dstH.rearrange("(pr hh) p n -> (hh p) pr n", hh=2), stg.t[:].rearrange("p (a n) -> p a n", n=128), stg,
                R=[stg.b])

        def ck(name):
            if stage is not None and name == stage:
                raise _Stop()

        def body():
          ck("const")
          if do_sample:
              NT = NSEQ * TS
              dma("sp", xres.t[:, :, 0:NT], xsT.rearrange("(t p) c -> p t c", p=128), xres, W=[xres.b])
              for l in range(2):
                  load_qkvo(l)
                  dma("sp", nrow.t[0:64, :], sn[l], nrow, W=[nrow.b])
                  p = pss()
                  for kt in range(2):
                      mm(p, p.t[:, kt * 64:(kt + 1) * 64], nrow.t[0:64, kt * 128:(kt + 1) * 128], I_f[0:64, 0:64], True, True,
                         [nrow.b, CS.b])
                  cp(nT_all.t[:, :, :], p.t[:, 0:128].rearrange("p (kt a) -> p kt a", a=64), [p.b], [nT_all.b])
                  dma("sp", m_all.t[:, :], smT[l], m_all, W=[m_all.b])
                  for (src, buf, ntile) in ((smc, xmin, 8), (ssc, sbcin, 24)):
                      for ch in range(ntile // 8):
                          dma("sp", rowstg.t[0:48, :], src[l][:, ch * 1024:(ch + 1) * 1024], rowstg, W=[rowstg.b])
                          for i8 in range(8):
                              i = ch * 8 + i8
                              p = pss()
                              mm(p, p.t[:, 0:48], rowstg.t[0:48, i8 * 128:(i8 + 1) * 128], I_f[0:48, 0:48], True, True,
                                 [rowstg.b, CS.b])
                              cp(conv_views(buf, i, NSEQ, TS)[:, :, 0:3], p.t[:, 0:48].rearrange("p (j r) -> p j r", r=3),
                                 [p.b], [buf.b])
                  ck("hist")
                  norm_to_hT(l, NT)
                  ck("norm")
                  in_proj(l, NT, NSEQ, TS, ("xm", "zm"))
                  conv(l, NT, NSEQ, TS, ("xm",))
                  in_proj(l, NT, NSEQ, TS, ("zs", "xbc"))
                  conv(l, NT, NSEQ, TS, ("xbc",))
                  in_proj(l, NT, NSEQ, TS, ("dt", "u", "v", "zc"))
                  ck("conv")
                  s_ = S[0]
                  for j in range(NSEQ):
                      load_state(s_, sC[l, j], sssm[l, j], j, stgS)
                      ck("load")
                      run_il([("mlstm", lambda: mlstm_seq(l, [j * TS], TS, s_)),
                              ("ssd", lambda: ssd_chunk(l, j * TS, TS, s_)),
                              ("cmlp", lambda: cmlp_chunk(l, j * TS, TS, cv_dst=ocv[l, j]))])
                      ck("cmlp")
                      store_state(s_, oC[l, j], on[l, j], om[l, j], ossm[l, j], stgO)
                      ck("store")
                  conv_state_out(l, NT, "s")
                  out_proj(l, NT)
                  ck("outproj")
              final_out(NT, ysT)
              ck("sfinal")
              c.barrier()
              phase["p"] = "prompt"

          for l in range(2):
              c.op("dve", lambda e, l=l: e.memset(S[l].CT.t[:], 0.0), writes=[S[l].CT.b])
              c.op("dve", lambda e, l=l: e.memset(S[l].HT.t[:], 0.0), writes=[S[l].HT.b])
              c.op("dve", lambda e, l=l: e.memset(S[l].m.t[:], 0.0), writes=[S[l].m.b])
              c.op("dve", lambda e, l=l: e.memset(mh[l].t[:], 0.0), writes=[mh[l].b])
              c.op("dve", lambda e, l=l: e.memset(sh[l].t[:], 0.0), writes=[sh[l].b])
          NT = 128
          for b in range(NBLK):
              dma("sp", xres.t[:, :, 0:NT], xT.rearrange("(t p) c -> p t c", p=128)[:, :, b * 128:(b + 1) * 128], xres,
                  W=[xres.b])
              for l in range(2):
                  load_qkvo(l)
                  norm_to_hT(l, NT)
                  cp(xmin.t[:, :, 0:3], mh[l].t[:, :, :], [mh[l].b], [xmin.b])
                  cp(sbcin.t[:, :, 0:3], sh[l].t[:, :, :], [sh[l].b], [sbcin.b])
                  ck("p_norm")
                  in_proj(l, NT, 1, 128, ("xm", "zm"))
                  conv(l, NT, 1, 128, ("xm",))
                  in_proj(l, NT, 1, 128, ("zs", "xbc"))
                  conv(l, NT, 1, 128, ("xbc",))
                  in_proj(l, NT, 1, 128, ("dt", "u", "v", "zc"))
                  ck("p_conv")
                  cp(mh[l].t[:, :, :], xmin.t[:, :, 128:131], [xmin.b], [mh[l].b])
                  cp(sh[l].t[:, :, :], sbcin.t[:, :, 128:131], [sbcin.b], [sh[l].b])
                  if b == NBLK - 1:
                      conv_state_out(l, NT, "p")
                  run_il([("mlstm", lambda: mlstm_seq(l, [0, 64], 64, S[l])),
                          ("ssd", lambda: ssd_chunk(l, 0, 128, S[l])),
                          ("cmlp", lambda: cmlp_chunk(l, 0, 128))])
                  ck("p_cmlp")
                  out_proj(l, NT)
                  ck("p_outproj")
              final_out(NT, yT[:, b * 128:(b + 1) * 128])
          for l in range(2):
              store_state(S[l], pC[l], pn[l], pm[l], pssm[l], stgP)

        try:
            body()
        except _Stop:
            pass
        c.barrier(final=True)
        c.emit()
    return nc, c.ninst


def make_consts():
    cst = np.zeros((128, 13, 128), np.float32)
    s = np.arange(128)[:, None]
    t = np.arange(128)[None, :]
    cst[:, 0, :] = np.eye(128, dtype=np.float32)
    cst[:, 1, :] = (s <= t)
    cst[:, 2, :] = np.where(s > t, 30000.0, 0.0)
    cst[:, 3, :] = np.where(s > t, -30000.0, 0.0)
    cst[:, 4, :] = 1.0
    cst[127, 5, :] = 1.0
    cst[63, 6, :] = 1.0
    cst[3, 7, :] = 1.0
    for h in range(4):
        cst[h, 9 + h, :] = 1.0
    return cst


_NC_CACHE = {}


def kernel(x_prompt, x_sample, state_mlstm_C, state_mlstm_n, state_mlstm_m, state_mlstm_conv,
           state_ssm, state_ssm_conv, norm_g, w_in, m_conv_w, m_conv_b, m_w_qk, m_w_vo, m_b_o,
           m_w_gate, m_b_gate, m_norm_g, s_conv_w, s_conv_b, s_dt_bias, s_A_log, s_D, s_norm_g,
           c_v_norm_g, c_w_s, c_b_s, w_out, final_norm_g, _ncores=8):
    f = lambda a: np.ascontiguousarray(np.asarray(a, dtype=np.float32))
    x_prompt = f(x_prompt)
    x_sample = f(x_sample)
    BATCH, SEQ, _ = x_prompt.shape
    DEC = x_sample.shape[0]
    NBLK = SEQ // 128
    ncores = _ncores
    assert DEC == NSEQ * ncores
    if NBLK not in _NC_CACHE:
        _NC_CACHE[NBLK] = build(NBLK)[0]
    nc = _NC_CACHE[NBLK]
    col = lambda a, nt: np.ascontiguousarray(np.asarray(a, np.float32).reshape(nt, 128).T)
    ng, fg = np.asarray(norm_g, np.float32), np.asarray(final_norm_g, np.float32)
    gcol_h = np.stack([col(ng[0], 16), col(ng[1], 16), col(fg, 16)], axis=1)
    mcw, mcb = np.asarray(m_conv_w, np.float32), np.asarray(m_conv_b, np.float32)
    scw, scb = np.asarray(s_conv_w, np.float32), np.asarray(s_conv_b, np.float32)
    cwm_h = np.stack([np.stack([col(mcw[l, k], 8) for k in range(4)] + [col(mcb[l], 8)], axis=-1) for l in range(2)], axis=1)
    cws_h = np.stack([np.stack([col(scw[l, k], 24) for k in range(4)] + [col(scb[l], 24)], axis=-1) for l in range(2)], axis=1)
    mng_h = np.stack([col(np.asarray(m_norm_g)[l], 8) for l in range(2)], axis=1)
    sng_h = np.stack([col(np.asarray(s_norm_g)[l], 16) for l in range(2)], axis=1)
    mbo_h = np.stack([col(np.asarray(m_b_o)[l], 8) for l in range(2)], axis=1)
    shared = {
        "gcol_h": f(gcol_h), "cwm_h": f(cwm_h), "cws_h": f(cws_h), "mng_h": f(mng_h), "sng_h": f(sng_h), "mbo_h": f(mbo_h),
        "w_in": f(w_in), "m_w_qk": f(m_w_qk), "m_w_vo": f(m_w_vo), "m_w_gate": f(m_w_gate),
        "m_b_gate": f(m_b_gate), "s_dt_bias": f(s_dt_bias), "s_A_log": f(s_A_log), "s_D": f(s_D),
        "c_v_norm_g": f(c_v_norm_g), "c_w_sT": f(np.transpose(np.asarray(c_w_s), (0, 1, 3, 2))),
        "c_b_s": f(c_b_s), "w_out": f(w_out), "cst": make_consts(),
    }
    sC_, sn_, sm_, smc_, sss_, ssc_ = (f(state_mlstm_C), f(state_mlstm_n), f(state_mlstm_m), f(state_mlstm_conv),
                                       f(state_ssm), f(state_ssm_conv))
    in_maps = []
    for cid in range(ncores):
        sl = slice(cid * NSEQ, (cid + 1) * NSEQ)
        m = dict(shared)
        m["xT"] = f(x_prompt[cid % BATCH].T)
        m["xsT"] = f(x_sample[sl].reshape(NSEQ * TS, D).T)
        m["sC"] = f(sC_[:, sl])
        m["sn"] = f(sn_[:, sl].reshape(2, NSEQ * 4, 256))
        m["smT"] = f(np.transpose(sm_[:, sl], (0, 2, 1)))
        m["smc"] = f(smc_[:, sl].reshape(2, NSEQ * 3, 1024))
        m["sssm"] = f(sss_[:, sl])
        m["ssc"] = f(ssc_[:, sl].reshape(2, NSEQ * 3, 3072))
        in_maps.append(m)
    res = run_bass_kernel_spmd(nc, in_maps, core_ids=list(range(ncores)))
    R = res.results
    nb = min(BATCH, ncores)
    y_prompt = np.stack([R[i]["yT"].T for i in range(nb)])
    y_sample = np.concatenate([R[i]["ysT"].T.reshape(NSEQ, TS, D) for i in range(ncores)])
    pst = lambda k: np.stack([R[i][k] for i in range(nb)], axis=1)
    sst = lambda k: np.concatenate([R[i][k] for i in range(ncores)], axis=1)
    outs = (y_prompt, y_sample, pst("pC"), pst("pn"), pst("pm"), pst("pmc"), pst("pssm"), pst("psc"),
            sst("oC"), sst("on"), sst("om"), sst("omc"), sst("ossm"), sst("osc"), sst("ocv"))
    return tuple(np.ascontiguousarray(o, dtype=np.float32) for o in outs)
```

```python
import contextlib
import numpy as np
import concourse.bass as bass
import concourse.mybir as mybir
from concourse.bass_utils import run_bass_kernel_spmd

F32 = mybir.dt.float32
BF16 = mybir.dt.bfloat16
ALU = mybir.AluOpType
AF = mybir.ActivationFunctionType
AX = mybir.AxisListType

D = 2048
NDT = 16
EPS = 1e-6
NSEQ = 16
TS = 4
SEGS = (("xm", 0, 1024), ("zm", 1024, 1024), ("zs", 2048, 2048), ("xbc", 4096, 3072),
        ("dt", 7168, 32), ("u", 7200, 1024), ("v", 8224, 1024), ("zc", 9248, 1024))
PROJ_W = 10272
GW = 256
WSPLIT = 4


class Buf:
    __slots__ = ("name", "last_write", "readers", "dma_sem", "dma_cnt", "excl")

    def __init__(self, name, excl=False):
        self.name = name
        self.excl = excl
        self.last_write = None
        self.readers = []
        self.dma_sem = None
        self.dma_cnt = 0


class TT:
    __slots__ = ("t", "b")

    def __init__(self, t, b):
        self.t = t
        self.b = b


class Ctx:
    ENG = ("pe", "dve", "act", "pool", "sp")

    def __init__(self, nc, stack):
        self.nc = nc
        self.stack = stack
        self.q = {e: [] for e in self.ENG}
        self.sem = {e: stack.enter_context(nc.semaphore("s_" + e)) for e in self.ENG}
        self.cnt = {e: 0 for e in self.ENG}
        self.seen = {e: {} for e in self.ENG}
        self.dma_bufs = []
        self.ninst = 0
        self.rec = None

    def _collect(self, e, reads, writes):
        need = {}

        def add(tok):
            if tok is None:
                return
            s, v = tok
            k = id(s)
            if need.get(k, (None, 0))[1] < v:
                need[k] = (s, v)
        own_sem = self.sem[e]
        for b in reads:
            add(b.last_write)
            if b.excl:
                for t in b.readers:
                    if t[0] is not own_sem:
                        add(t)
        for b in writes:
            add(b.last_write)
            for t in b.readers:
                add(t)
        waits = []
        seen = self.seen[e]
        own = id(self.sem[e])
        for k, (s, v) in need.items():
            if seen.get(k, 0) >= v:
                continue
            if e == "pe" and k == own:
                continue
            seen[k] = v
            waits.append((s, v))
        return waits

    def _commit(self, tok, reads, writes):
        for b in writes:
            b.last_write = tok
            b.readers = []
        for b in reads:
            b.readers.append(tok)
            if len(b.readers) > 48:
                best = {}
                for s, v in b.readers:
                    if best.get(id(s), (None, 0))[1] < v:
                        best[id(s)] = (s, v)
                b.readers = list(best.values())

    def op(self, e, fn, reads=(), writes=(), inc=True):
        if self.rec is not None:
            self.rec.append(("op", e, fn, reads, writes, inc))
            return
        waits = self._collect(e, reads, writes)
        sem = self.sem[e]
        if inc:
            self.cnt[e] += 1
            tok = (sem, self.cnt[e])
        else:
            tok = (sem, self.cnt[e] + 1)
        self._commit(tok, reads, writes)
        self.ninst += 1 + len(waits)

        def run(eng, waits=waits, fn=fn, sem=sem, inc=inc):
            for s, v in waits:
                eng.wait_ge(s, v)
            if inc:
                fn(eng).then_inc(sem, 1)
            else:
                fn(eng)
        self.q[e].append(run)

    def dma(self, e, out, in_, sb, reads=(), writes=(), **kw):
        if self.rec is not None:
            self.rec.append(("dma", e, (out, in_, sb, kw), reads, writes, None))
            return
        waits = self._collect(e, reads, writes)
        if sb.dma_sem is None:
            sb.dma_sem = self.stack.enter_context(self.nc.semaphore("d_" + sb.name))
            self.dma_bufs.append(sb)
        sb.dma_cnt += 16
        sem = sb.dma_sem
        tok = (sem, sb.dma_cnt)
        self._commit(tok, reads, writes)
        self.ninst += 1 + len(waits)

        def run(eng, waits=waits, sem=sem, out=out, in_=in_, kw=kw):
            for s, v in waits:
                eng.wait_ge(s, v)
            eng.dma_start(out=out, in_=in_, **kw).then_inc(sem, 16)
        self.q[e].append(run)

    def barrier(self, final=False):
        toks = [(b.dma_sem, b.dma_cnt) for b in self.dma_bufs]
        toks += [(self.sem[e], self.cnt[e]) for e in self.ENG if self.cnt[e] > 0]
        for e in (("sp",) if final else self.ENG):
            waits = []
            for s, v in toks:
                if self.seen[e].get(id(s), 0) >= v:
                    continue
                if s is self.sem[e]:
                    continue
                self.seen[e][id(s)] = v
                waits.append((s, v))

            def run(eng, waits=waits):
                for s, v in waits:
                    eng.wait_ge(s, v)
            self.q[e].append(run)

    def emit(self):
        with self.nc.Block() as block:
            @block.tensor
            def _(eng):
                for f in self.q["pe"]:
                    f(eng)

            @block.vector
            def _(eng):
                for f in self.q["dve"]:
                    f(eng)

            @block.scalar
            def _(eng):
                for f in self.q["act"]:
                    f(eng)

            @block.gpsimd
            def _(eng):
                for f in self.q["pool"]:
                    f(eng)

            @block.sync
            def _(eng):
                for f in self.q["sp"]:
                    f(eng)


class _Stop(Exception):
    pass


def build(NBLK, do_sample=True, stage=None):
    nc = bass.Bass("TRN2", target_bir_lowering=False)
    TP = NBLK * 128

    def din(name, shape):
        return nc.dram_tensor(name, list(shape), F32, kind="ExternalInput").ap()

    def dout(name, shape):
        return nc.dram_tensor(name, list(shape), F32, kind="ExternalOutput").ap()

    xT = din("xT", [D, TP])
    xsT = din("xsT", [D, 64])
    sC = din("sC", [2, NSEQ, 4, 256, 256])
    sn = din("sn", [2, NSEQ * 4, 256])
    smc = din("smc", [2, NSEQ * 3, 1024])
    sssm = din("sssm", [2, NSEQ, 32, 64, 128])
    ssc = din("ssc", [2, NSEQ * 3, 3072])
    gcol_h = din("gcol_h", [128, 3, NDT])
    cwm_h = din("cwm_h", [128, 2, 8, 5])
    cws_h = din("cws_h", [128, 2, 24, 5])
    mng_h = din("mng_h", [128, 2, 8])
    sng_h = din("sng_h", [128, 2, 16])
    mbo_h = din("mbo_h", [128, 2, 8])
    smT = din("smT", [2, 4, NSEQ])
    w_in = din("w_in", [2, D, PROJ_W])
    m_w_qk = din("m_w_qk", [2, 4, 256, 512])
    m_w_vo = din("m_w_vo", [2, 4, 256, 512])
    m_w_gate = din("m_w_gate", [2, 3072, 8])
    m_b_gate = din("m_b_gate", [2, 8])
    s_dt_bias = din("s_dt_bias", [2, 32])
    s_A_log = din("s_A_log", [2, 32])
    s_D = din("s_D", [2, 32])
    c_v_norm_g = din("c_v_norm_g", [2, 1024])
    c_w_sT = din("c_w_sT", [2, 4, 128, 128])
    c_b_s = din("c_b_s", [2, 4, 128])
    w_out = din("w_out", [2, 4096, D])
    cst = din("cst", [128, 13, 128])

    yT = dout("yT", [D, TP])
    ysT = dout("ysT", [D, 64])
    pC = dout("pC", [2, 4, 256, 256])
    pn = dout("pn", [2, 4, 256])
    pm = dout("pm", [2, 4])
    pmc = dout("pmc", [2, 3, 1024])
    pssm = dout("pssm", [2, 32, 64, 128])
    psc = dout("psc", [2, 3, 3072])
    oC = dout("oC", [2, NSEQ, 4, 256, 256])
    on = dout("on", [2, NSEQ, 4, 256])
    om = dout("om", [2, NSEQ, 4])
    omc = dout("omc", [2, NSEQ, 3, 1024])
    ossm = dout("ossm", [2, NSEQ, 32, 64, 128])
    osc = dout("osc", [2, NSEQ, 3, 3072])
    ocv = dout("ocv", [2, NSEQ, 4, 1024])

    NGI = sum((n_ + GW - 1) // GW for (_, _, n_) in SEGS)
    wscr_in = nc.dram_tensor("wscr_in", [2, NGI, 128, 4096], BF16, kind="Internal").ap()
    wscr_out = nc.dram_tensor("wscr_out", [2, NDT, 128, 4096], BF16, kind="Internal").ap()
    phase = {"p": "sample" if do_sample else "none"}

    with contextlib.ExitStack() as st:
        c = Ctx(nc, st)

        def sb(name, shape, dt=F32):
            return TT(st.enter_context(nc.sbuf_tensor(name, list(shape), dt)), Buf(name))

        big = []
        small = []
        for i in range(4):
            t = st.enter_context(nc.psum_tensor("pb%d" % i, [128, 512], F32))
            big.append(TT(t, Buf("pb%d" % i, excl=True)))
        for i in range(4):
            t = st.enter_context(nc.psum_tensor("ph%d" % i, [128, 512], F32))
            small.append(TT(t, Buf("ph%d" % i, excl=True)))
        rot = {"b": 0, "s": 0, "w": 0, "e": 0}

        cpool = {"p": None}
        allbanks = big + small
        CH_POOLS = {"mlstm": {"big": allbanks[0:2], "small": allbanks[2:3]},
                    "ssd": {"big": allbanks[3:5], "small": allbanks[5:6]},
                    "cmlp": {"big": allbanks[6:7], "small": allbanks[7:8]},
                    "inproj": {"big": allbanks[3:4], "small": allbanks[3:8]}}

        def psb():
            if cpool["p"] is not None:
                pl = cpool["p"]
                pl["bi"] = (pl.get("bi", 0) + 1) % len(pl["big"])
                return pl["big"][pl["bi"]]
            rot["b"] = (rot["b"] + 1) % 4
            return big[rot["b"]]

        def pss():
            if cpool["p"] is not None:
                pl = cpool["p"]
                pl["si"] = (pl.get("si", 0) + 1) % len(pl["small"])
                return pl["small"][pl["si"]]
            rot["s"] = (rot["s"] + 1) % 4
            return small[rot["s"]]

        def mm(p, pap, lhsT, rhs, start, stop, R):
            c.op("pe", lambda e: e.matmul(pap, lhsT, rhs, start=start, stop=stop), reads=R, writes=[p.b])

        def act(out, in_, func, R, W, **kw):
            c.op("act", lambda e: e.activation(out, in_, func, **kw), reads=R, writes=W)

        def tt(out, a, b_, op, R, W, eng="dve"):
            c.op(eng, lambda e: e.tensor_tensor(out, a, b_, op), reads=R, writes=W)

        def ts(out, a, s1, s2, op0, op1, R, W, eng="dve"):
            if op1 is None:
                c.op(eng, lambda e: e.tensor_scalar(out, a, s1, None, op0), reads=R, writes=W)
            else:
                c.op(eng, lambda e: e.tensor_scalar(out, a, s1, s2, op0, op1), reads=R, writes=W)

        def stt(out, a, s, b_, op0, op1, R, W):
            c.op("dve", lambda e: e.scalar_tensor_tensor(out, a, s, b_, op0, op1), reads=R, writes=W)

        def cp(out, in_, R, W):
            rot["e"] ^= 1
            if rot["e"]:
                c.op("act", lambda e: e.activation(out, in_, AF.Copy), reads=R, writes=W)
            else:
                c.op("dve", lambda e: e.tensor_copy(out, in_), reads=R, writes=W)

        def dma(e, out, in_, sbuf, R=(), W=()):
            c.dma(e, out, in_, sbuf.b, reads=R, writes=W)

        CS = sb("CS", [128, 13, 128])
        dma("sp", CS.t[:], cst, CS, W=[CS.b])
        I_f = CS.t[:, 0, :]
        U_f = CS.t[:, 1, :]
        MPOS = CS.t[:, 2, :]
        MNEG = CS.t[:, 3, :]
        ONES = CS.t[:, 4, :]
        SELL = {128: CS.t[:, 5, :], 64: CS.t[:, 6, :], 4: CS.t[:, 7, :]}
        SEL4 = [CS.t[:, 9 + h, :] for h in range(4)]
        I_b = sb("I_b", [128, 128], BF16)
        cp(I_b.t[:], I_f, [CS.b], [I_b.b])
        CB = [CS.b, I_b.b]

        gcol = sb("gcol", [128, 3, NDT])
        cwm = sb("cwm", [128, 2, 8, 5])
        cws = sb("cws", [128, 2, 24, 5])
        mng = sb("mng", [128, 2, 8])
        sng = sb("sng", [128, 2, 16])
        mbo = sb("mbo", [128, 2, 8])
        for (tl, src) in ((gcol, gcol_h), (cwm, cwm_h), (cws, cws_h), (mng, mng_h), (sng, sng_h), (mbo, mbo_h)):
            dma("sp", tl.t[:], src, tl, W=[tl.b])
        cvg_bc = sb("cvg_bc", [128, 1024])
        bg_bc = sb("bg_bc", [128, 2, 8])
        dtb_bc = sb("dtb_bc", [128, 2, 32])
        A_bc = sb("A_bc", [128, 2, 32])
        D_bc = sb("D_bc", [128, 2, 32])
        cb_bc = sb("cb_bc", [128, 2, 4, 128])
        for l in range(2):
            dma("sp", bg_bc.t[:, l, :], m_b_gate[l].partition_broadcast(128), bg_bc, W=[bg_bc.b])
            dma("sp", dtb_bc.t[:, l, :], s_dt_bias[l].partition_broadcast(128), dtb_bc, W=[dtb_bc.b])
            dma("sp", A_bc.t[:, l, :], s_A_log[l].partition_broadcast(128), A_bc, W=[A_bc.b])
            dma("sp", D_bc.t[:, l, :], s_D[l].partition_broadcast(128), D_bc, W=[D_bc.b])
            dma("sp", cb_bc.t[:, l, :, :], c_b_s[l].partition_broadcast(128), cb_bc, W=[cb_bc.b])
        act(A_bc.t[:], A_bc.t[:], AF.Exp, [A_bc.b], [A_bc.b])
        ts(A_bc.t[:], A_bc.t[:], -1.0, None, ALU.mult, None, [A_bc.b], [A_bc.b])
        wg = sb("wg", [128, 2, 24, 8], BF16)
        WmT = sb("WmT", [128, 2, 4, 128], BF16)
        wtmp = sb("wtmp", [128, 4, 128])
        for l in range(2):
            dma("pool", wg.t[:, l, :, :], m_w_gate[l].rearrange("(a p) g -> p a g", p=128), wg, W=[wg.b])
            dma("sp", wtmp.t[:], c_w_sT[l].rearrange("g s t -> s g t"), wtmp, W=[wtmp.b])
            for g in range(4):
                tt(WmT.t[:, l, g, :], wtmp.t[:, g, :], U_f, ALU.mult, [wtmp.b, CS.b], [WmT.b])

        xres = sb("xres", [128, NDT, 128])
        hT = sb("hT", [128, NDT, 128], BF16)
        sq = [sb("sq%d" % i, [128, 128]) for i in range(2)]
        rstd_bc = sb("rstd_bc", [128, 128])
        wslot = [sb("wslot%d" % i, [128, 32 * 128], BF16) for i in range(2)]
        wqk = sb("wqk", [128, 8, 512], BF16)
        wvo = sb("wvo", [128, 8, 512], BF16)
        xmin = sb("xmin", [128, 8, 131])
        sbcin = sb("sbcin", [128, 24, 131])
        xmT = sb("xmT", [128, 8, 128], BF16)
        zmS = sb("zmS", [128, 8, 128], BF16)
        xmcT = sb("xmcT", [128, 8, 128], BF16)
        zsS = sb("zsS", [128, 16, 128])
        xsT_ = sb("xsT_", [128, 16, 128])
        BT = sb("BT", [128, 4, 128], BF16)
        CTT = sb("CTT", [128, 4, 128], BF16)
        dtR = sb("dtR", [32, 128])
        uT = sb("uT", [128, 8, 128])
        vT = sb("vT", [128, 8, 128])
        zcS = sb("zcS", [128, 8, 128], BF16)
        mixT = sb("mixT", [128, 32, 128], BF16)
        cacc = sb("cacc", [128, 128])
        mh = [sb("mh%d" % l, [128, 8, 3]) for l in range(2)]
        sh = [sb("sh%d" % l, [128, 24, 3]) for l in range(2)]

        class State:
            pass
        S = []
        for l in range(2):
            s_ = State()
            s_.CT = sb("CT%d" % l, [128, 4, 2, 257])
            s_.HT = sb("HT%d" % l, [128, 32, 64])
            s_.m = sb("m%d" % l, [4, 1])
            S.append(s_)
        CTb = sb("CTb", [128, 4, 2, 257], BF16)
        stgS = TT(S[1].HT.t[:].rearrange("p h d -> p (h d)"), S[1].HT.b)
        stgO = TT(S[1].CT.t[:].rearrange("p a b c -> p (a b c)")[:, 0:2048], S[1].CT.b)
        HTb = sb("HTb", [128, 8, 64], BF16)
        nT_all = sb("nT_all", [128, 2, 64])
        m_all = sb("m_all", [4, NSEQ])
        nrow = sb("nrow", [64, 256])

        stgP = TT(zsS.t[:].rearrange("p a b -> p (a b)"), zsS.b)
        qT = sb("qT", [128, 4, 2, 64], BF16)
        kT = sb("kT", [128, 4, 2, 64], BF16)
        vT_ = sb("vT_", [128, 4, 2, 64], BF16)
        ks = sb("ks", [64, 4, 256], BF16)
        v1 = sb("v1", [64, 4, 257], BF16)
        sigoT = sb("sigoT", [128, 2, 64], BF16)
        gT = sb("gT", [64, 8])
        nlf = sb("nlf", [64, 4])
        g_T = sb("g_T", [64, 4])
        nb_T = sb("nb_T", [64, 4])
        cm = [sb("cm%d" % i, [4, 64]) for i in range(2)]
        nbR = sb("nbR", [4, 64])
        MR = sb("MR", [4, 64])
        RW = sb("RW", [4, 2, 64])
        tmpR = sb("tmpR", [4, 64])
        mnew = sb("mnew", [4, 1])
        cols_T = sb("cols_T", [64, 8])
        wli = sb("wli", [128, 4])
        wT = sb("wT", [64, 64])
        SwT = sb("SwT", [64, 64], BF16)
        tmpA = sb("tmpA", [64, 257])
        nd = sb("nd", [64, 257])
        dn = sb("dn", [64, 1])
        hh = sb("hh", [64, 256])
        junk = sb("junk", [128, 256], BF16)
        ssq = sb("ssq", [128, 4])
        junk2 = sb("junk2", [128, 256], BF16)
        ssq2 = sb("ssq2", [128, 4])
        ssq3 = sb("ssq3", [128, 4])
        hn = sb("hn", [64, 256], BF16)
        wlv = sb("wlv", [64, 257], BF16)
        dt_T = sb("dt_T", [128, 32])
        a_T = sb("a_T", [128, 32])
        cs_T = sb("cs_T", [128, 32])
        ncs_T = sb("ncs_T", [128, 32])
        expcs_T = sb("expcs_T", [128, 32])
        wend = sb("wend", [128, 32])
        declast = sb("declast", [128, 32])
        B_T = sb("B_T", [128, 512], BF16)
        xs_T = sb("xs_T", [128, 512])
        xdt = sb("xdt", [128, 512], BF16)
        wx = sb("wx", [128, 512], BF16)
        zs_T = sb("zs_T", [128, 512])
        CBs = sb("CBs", [128, 128])
        A1h = sb("A1h", [128, 512])
        decTh = sb("decTh", [128, 512])
        MTh = sb("MTh", [128, 512], BF16)
        yt = sb("yt", [128, 512])
        yt2 = sb("yt2", [128, 512])
        yo = sb("yo", [128, 512], BF16)
        vn = sb("vn", [128, 1024])
        rowstg = vn
        vnb = sb("vnb", [128, 1024], BF16)
        tmpc = sb("tmpc", [128, 128])

        c.op("dve", lambda e: e.memset(v1.t[:, :, 256:257], 1.0), writes=[v1.b])

        def load_w(src_ap, ktiles, width, scr):
            rot["w"] ^= 1
            slot = wslot[rot["w"]]
            n = ktiles * width
            view = slot.t[:, 0:n].rearrange("p (t c) -> p t c", c=width)
            if phase["p"] == "prompt":
                dma("sp", slot.t[:, 0:n], scr[:, 0:n], slot, W=[slot.b])
                return slot, view
            srcv = src_ap.rearrange("(t p) c -> p t c", p=128)
            for t0 in range(0, ktiles, WSPLIT):
                dma("pool", view[:, t0:t0 + WSPLIT, :], srcv[:, t0:t0 + WSPLIT, :], slot, W=[slot.b])
            if phase["p"] == "sample":
                dma("sp", scr[:, 0:n], slot.t[:, 0:n], slot, R=[slot.b])
            return slot, view

        def rmsnorm_stats(NT, src):
            p = pss()
            for i in range(NDT):
                s_ = sq[i % 2]
                act(s_.t[:, 0:NT], src.t[:, i, 0:NT], AF.Square, [src.b], [s_.b])
                mm(p, p.t[:, 0:NT], ONES, s_.t[:, 0:NT], i == 0, i == NDT - 1, [CS.b, s_.b])
            act(rstd_bc.t[:, 0:NT], p.t[:, 0:NT], AF.Ln, [p.b], [rstd_bc.b], scale=1.0 / D, bias=EPS)
            act(rstd_bc.t[:, 0:NT], rstd_bc.t[:, 0:NT], AF.Exp, [rstd_bc.b], [rstd_bc.b], scale=-0.5)

        def conv_views(buf, i, nseq, T):
            return buf.t[:, i, 0:nseq * (3 + T)].rearrange("p (j w) -> p j w", w=3 + T)

        def tokview(ap2d, nseq, T):
            return ap2d.rearrange("p (j t) -> p j t", t=T)

        GI0 = {}
        _g = 0
        for (nm_, c0_, n_) in SEGS:
            GI0[nm_] = _g
            _g += (n_ + GW - 1) // GW

        def in_proj(l, NT, nseq, T, segs=None):
            for (nm, c0, ncols) in SEGS:
                if segs is not None and nm not in segs:
                    continue
                gi = GI0[nm] - 1
                g0 = 0
                while g0 < ncols:
                    gw = min(GW, ncols - g0)
                    gi += 1
                    slot, wv = load_w(w_in[l][:, c0 + g0:c0 + g0 + gw], NDT, gw, wscr_in[l, gi])
                    for t0 in range(0, gw, 128):
                        tw = min(128, gw - t0)
                        i = (g0 + t0) // 128
                        p = pss()
                        for k in range(NDT):
                            mm(p, p.t[0:tw, 0:NT], wv[:, k, t0:t0 + tw], hT.t[:, k, 0:NT], k == 0, k == NDT - 1,
                               [slot.b, hT.b])
                        pin = p.t[0:tw, 0:NT]
                        if nm == "xm":
                            cp(conv_views(xmin, i, nseq, T)[:, :, 3:3 + T], tokview(pin, nseq, T), [p.b], [xmin.b])
                            cp(xmT.t[:, i, 0:NT], pin, [p.b], [xmT.b])
                        elif nm == "zm":
                            act(zmS.t[:, i, 0:NT], pin, AF.Silu, [p.b], [zmS.b])
                        elif nm == "zs":
                            act(zsS.t[:, i, 0:NT], pin, AF.Silu, [p.b], [zsS.b])
                        elif nm == "xbc":
                            cp(conv_views(sbcin, i, nseq, T)[:, :, 3:3 + T], tokview(pin, nseq, T), [p.b], [sbcin.b])
                        elif nm == "dt":
                            cp(dtR.t[0:32, 0:NT], pin, [p.b], [dtR.b])
                        elif nm == "u":
                            cp(uT.t[:, i, 0:NT], pin, [p.b], [uT.b])
                        elif nm == "v":
                            cp(vT.t[:, i, 0:NT], pin, [p.b], [vT.b])
                        elif nm == "zc":
                            act(zcS.t[:, i, 0:NT], pin, AF.Silu, [p.b], [zcS.b])
                    g0 += gw
                ck("seg_" + nm)

        def conv(l, NT, nseq, T, which=("xm", "xbc")):
            for (buf, wts, ntile) in ((xmin, cwm, 8), (sbcin, cws, 24)):
                if ("xm" if buf is xmin else "xbc") not in which:
                    continue
                for i in range(ntile):
                    xin = conv_views(buf, i, nseq, T)
                    acc = tokview(cacc.t[:, 0:NT], nseq, T)
                    ts(acc, xin[:, :, 0:T], wts.t[:, l, i, 0:1], wts.t[:, l, i, 4:5], ALU.mult, ALU.add,
                       [buf.b, wts.b], [cacc.b])
                    for k in range(1, 4):
                        stt(acc, xin[:, :, k:k + T], wts.t[:, l, i, k:k + 1], acc, ALU.mult, ALU.add,
                            [buf.b, wts.b, cacc.b], [cacc.b])
                    if buf is xmin:
                        dst = xmcT
                        o = xmcT.t[:, i, 0:NT]
                    elif i < 16:
                        dst = xsT_
                        o = xsT_.t[:, i, 0:NT]
                    elif i < 20:
                        dst = BT
                        o = BT.t[:, i - 16, 0:NT]
                    else:
                        dst = CTT
                        o = CTT.t[:, i - 20, 0:NT]
                    act(o, cacc.t[:, 0:NT], AF.Silu, [cacc.b], [dst.b])

        def conv_state_out(l, NT, kind):
            for (buf, ntile, dstp, dsts) in ((xmin, 8, pmc, omc), (sbcin, 24, psc, osc)):
                for ch in range(ntile // 8):
                    for half in range(2):
                        p = psb()
                        for j in range(4):
                            i = ch * 8 + half * 4 + j
                            if kind == "p":
                                src = buf.t[:, i, 3:3 + NT]
                            else:
                                src = buf.t[:, i, 0:NSEQ * 7].rearrange("p (j w) -> p j w", w=7)[:, :, 3:7]
                            if kind == "p":
                                mm(p, p.t[0:NT, j * 128:(j + 1) * 128], src, I_f, True, True, [buf.b, CS.b])
                            else:
                                cp(cacc.t[:, 0:NT].rearrange("p (j t) -> p j t", t=TS), src, [buf.b], [cacc.b])
                                mm(p, p.t[0:NT, j * 128:(j + 1) * 128], cacc.t[:, 0:NT], I_f, True, True,
                                   [cacc.b, CS.b])
                        cp(rowstg.t[0:NT, half * 512:(half + 1) * 512], p.t[0:NT, 0:512], [p.b], [rowstg.b])
                    cc = slice(ch * 1024, (ch + 1) * 1024)
                    if kind == "p":
                        dma("sp", dstp[l][:, cc], rowstg.t[NT - 3:NT, :], rowstg, R=[rowstg.b])
                    else:
                        for j in range(NSEQ):
                            dma("sp", dsts[l, j][:, cc], rowstg.t[j * 4 + 1:j * 4 + 4, :], rowstg, R=[rowstg.b])

        def mlstm_chunk(l, c0, L, s_):
            cs = slice(c0, c0 + L)
            for h in range(4):
                for (src, W, e0, dst) in ((xmcT, wqk, 0, qT), (xmcT, wqk, 256, kT), (xmT, wvo, 0, vT_)):
                    for et in range(2):
                        p = pss()
                        for dt in range(2):
                            mm(p, p.t[:, 0:L], W.t[:, h * 2 + dt, e0 + et * 128:e0 + (et + 1) * 128],
                               src.t[:, h * 2 + dt, cs], dt == 0, dt == 1, [W.b, src.b])
                        cp(dst.t[:, h, et, 0:L], p.t[:, 0:L], [p.b], [dst.b])
                p = pss()
                for dt in range(2):
                    mm(p, p.t[0:L, 0:256], xmcT.t[:, h * 2 + dt, cs], wqk.t[:, h * 2 + dt, 256:512], dt == 0, dt == 1,
                       [xmcT.b, wqk.b])
                act(ks.t[0:L, h, :], p.t[0:L, 0:256], AF.Copy, [p.b], [ks.b], scale=0.0625)
                p = pss()
                for dt in range(2):
                    mm(p, p.t[0:L, 0:256], xmT.t[:, h * 2 + dt, cs], wvo.t[:, h * 2 + dt, 0:256], dt == 0, dt == 1,
                       [xmT.b, wvo.b])
                cp(v1.t[0:L, h, 0:256], p.t[0:L, 0:256], [p.b], [v1.b])
                yield
            pg = pss()
            n = 0
            for h in range(4):
                for part, srcT in enumerate((qT, kT, vT_)):
                    for dt in range(2):
                        mm(pg, pg.t[0:L, 0:8], srcT.t[:, h, dt, 0:L], wg.t[:, l, (h * 3 + part) * 2 + dt, :],
                           n == 0, n == 23, [srcT.b, wg.b])
                        n += 1
            tt(gT.t[0:L, :], pg.t[0:L, 0:8], bg_bc.t[0:L, l, :], ALU.add, [pg.b, bg_bc.b], [gT.b])
            act(nlf.t[0:L, :], gT.t[0:L, 4:8], AF.Exp, [gT.b], [nlf.b], scale=-1.0)
            act(nlf.t[0:L, :], nlf.t[0:L, :], AF.Ln, [nlf.b], [nlf.b], bias=1.0)
            p = pss()
            mm(p, p.t[0:L, 0:4], U_f[0:L, 0:L], nlf.t[0:L, :], True, True, [CS.b, nlf.b])
            tt(g_T.t[0:L, :], p.t[0:L, 0:4], gT.t[0:L, 0:4], ALU.add, [p.b, gT.b], [g_T.b])
            cp(nb_T.t[0:L, :], p.t[0:L, 0:4], [p.b], [nb_T.b])
            p2 = pss()
            mm(p2, p2.t[0:4, 0:L], g_T.t[0:L, :], I_f[0:L, 0:L], True, True, [g_T.b, CS.b])
            mm(p2, p2.t[0:4, 64:64 + L], nb_T.t[0:L, :], I_f[0:L, 0:L], True, True, [nb_T.b, CS.b])
            cp(cm[0].t[:, 0:L], p2.t[0:4, 0:L], [p2.b], [cm[0].b])
            cp(nbR.t[:, 0:L], p2.t[0:4, 64:64 + L], [p2.b], [nbR.b])
            src = 0
            shf = 1
            while shf < L:
                dst = 1 - src
                cp(cm[dst].t[:, 0:shf], cm[src].t[:, 0:shf], [cm[src].b], [cm[dst].b])
                tt(cm[dst].t[:, shf:L], cm[src].t[:, shf:L], cm[src].t[:, 0:L - shf], ALU.max, [cm[src].b], [cm[dst].b])
                src = dst
                shf *= 2
            ts(MR.t[:, 0:L], cm[src].t[:, 0:L], s_.m.t[:, 0:1], None, ALU.max, None, [cm[src].b, s_.m.b], [MR.b])
            act(RW.t[:, 0, 0:L], MR.t[:, 0:L], AF.Exp, [MR.b, s_.m.b], [RW.b], scale=-1.0, bias=s_.m.t[:, 0:1])
            tt(tmpR.t[:, 0:L], nbR.t[:, 0:L], MR.t[:, 0:L], ALU.subtract, [nbR.b, MR.b], [tmpR.b])
            act(RW.t[:, 1, 0:L], tmpR.t[:, 0:L], AF.Exp, [tmpR.b], [RW.b])
            tt(mnew.t[:, 0:1], MR.t[:, L - 1:L], nbR.t[:, L - 1:L], ALU.subtract, [MR.b, nbR.b], [mnew.b])
            p3 = pss()
            mm(p3, p3.t[0:L, 0:4], RW.t[:, 0, 0:L], I_f[0:4, 0:4], True, True, [RW.b, CS.b])
            mm(p3, p3.t[0:L, 4:8], RW.t[:, 1, 0:L], I_f[0:4, 0:4], True, True, [RW.b, CS.b])
            cp(cols_T.t[0:L, :], p3.t[0:L, 0:8], [p3.b], [cols_T.b])
            p4 = pss()
            mm(p4, p4.t[:, 0:4], SELL[L][0:L, :], cols_T.t[0:L, 0:4], True, True, [CS.b, cols_T.b])
            cp(wli.t[:, :], p4.t[:, 0:4], [p4.b], [wli.b])
            act(CTb.t[:], s_.CT.t[:], AF.Copy, [s_.CT.b], [CTb.b])
            yield
            for h in range(4):
                p5 = pss()
                mm(p5, p5.t[0:L, 0:L], SEL4[h][0:4, 0:L], MR.t[:, 0:L], True, False, [CS.b, MR.b])
                mm(p5, p5.t[0:L, 0:L], I_f[0:L, 0:L], MPOS[0:L, 0:L], False, True, [CS.b])
                act(wT.t[0:L, 0:L], p5.t[0:L, 0:L], AF.Exp, [p5.b, g_T.b], [wT.b], scale=-1.0, bias=g_T.t[0:L, h:h + 1])
                p6 = pss()
                for dt in range(2):
                    mm(p6, p6.t[0:L, 0:L], kT.t[:, h, dt, 0:L], qT.t[:, h, dt, 0:L], dt == 0, dt == 1, [kT.b, qT.b])
                stt(SwT.t[0:L, 0:L], p6.t[0:L, 0:L], 0.0625, wT.t[0:L, 0:L], ALU.mult, ALU.mult, [p6.b, wT.b], [SwT.b])
                pa = psb()
                mm(pa, pa.t[0:L, 0:257], SwT.t[0:L, 0:L], v1.t[0:L, h, :], True, True, [SwT.b, v1.b])
                pb = psb()
                for kt in range(2):
                    mm(pb, pb.t[0:L, 0:257], qT.t[:, h, kt, 0:L], CTb.t[:, h, kt, :], kt == 0, kt == 1, [qT.b, CTb.b])
                act(tmpA.t[0:L, :], pa.t[0:L, 0:257], AF.Copy, [pa.b], [tmpA.b])
                stt(nd.t[0:L, :], pb.t[0:L, 0:257], cols_T.t[0:L, h:h + 1], tmpA.t[0:L, :], ALU.mult, ALU.add,
                    [pb.b, cols_T.b, tmpA.b], [nd.b])
                yield
                act(dn.t[0:L, :], nd.t[0:L, 256:257], AF.Abs, [nd.b], [dn.b])
                tt(dn.t[0:L, :], dn.t[0:L, :], cols_T.t[0:L, 4 + h:5 + h], ALU.max, [dn.b, cols_T.b], [dn.b])
                c.op("dve", lambda e: e.reciprocal(dn.t[0:L, :], dn.t[0:L, :]), reads=[dn.b], writes=[dn.b])
                for et in range(2):
                    p = pss()
                    for dt in range(2):
                        mm(p, p.t[:, 0:L], wvo.t[:, h * 2 + dt, 256 + et * 128:256 + (et + 1) * 128], xmT.t[:, h * 2 + dt, cs],
                           dt == 0, dt == 1, [wvo.b, xmT.b])
                    act(sigoT.t[:, et, 0:L], p.t[:, 0:L], AF.Sigmoid, [p.b, mbo.b], [sigoT.b],
                        bias=mbo.t[:, l, h * 2 + et:h * 2 + et + 1])
                po = pss()
                for et in range(2):
                    mm(po, po.t[0:L, et * 128:(et + 1) * 128], sigoT.t[:, et, 0:L], I_b.t[:, :], True, True, [sigoT.b, I_b.b])
                stt(hh.t[0:L, :], nd.t[0:L, 0:256], dn.t[0:L, 0:1], po.t[0:L, 0:256], ALU.mult, ALU.mult,
                    [nd.b, dn.b, po.b], [hh.b])
                act(junk.t[0:L, 0:256], hh.t[0:L, :], AF.Square, [hh.b], [junk.b, ssq.b], accum_out=ssq.t[0:L, 0:1])
                act(ssq.t[0:L, 0:1], ssq.t[0:L, 0:1], AF.Ln, [ssq.b], [ssq.b], scale=1.0 / 256, bias=EPS)
                act(ssq.t[0:L, 0:1], ssq.t[0:L, 0:1], AF.Exp, [ssq.b], [ssq.b], scale=-0.5)
                ts(hn.t[0:L, :], hh.t[0:L, :], ssq.t[0:L, 0:1], None, ALU.mult, None, [hh.b, ssq.b], [hn.b])
                for et in range(2):
                    p7 = pss()
                    mm(p7, p7.t[:, 0:L], hn.t[0:L, et * 128:(et + 1) * 128], I_b.t[0:L, 0:L], True, True, [hn.b, I_b.b])
                    stt(mixT.t[:, h * 2 + et, cs], p7.t[:, 0:L], mng.t[:, l, h * 2 + et:h * 2 + et + 1],
                        zmS.t[:, h * 2 + et, cs], ALU.mult, ALU.mult, [p7.b, mng.b, zmS.b], [mixT.b])
                yield
                ts(wlv.t[0:L, :], v1.t[0:L, h, :], wT.t[0:L, L - 1:L], None, ALU.mult, None, [v1.b, wT.b], [wlv.b])
                for kt in range(2):
                    p8 = psb()
                    mm(p8, p8.t[:, 0:257], ks.t[0:L, h, kt * 128:(kt + 1) * 128], wlv.t[0:L, :], True, True, [ks.b, wlv.b])
                    stt(s_.CT.t[:, h, kt, :], s_.CT.t[:, h, kt, :], wli.t[:, h:h + 1], p8.t[:, 0:257], ALU.mult, ALU.add,
                        [s_.CT.b, wli.b, p8.b], [s_.CT.b])
                yield
            cp(s_.m.t[:, :], mnew.t[:, :], [mnew.b], [s_.m.b])

        def ssd_chunk(l, c0, L, s_):
            cs = slice(c0, c0 + L)
            p = pss()
            mm(p, p.t[0:L, 0:32], dtR.t[0:32, cs], I_f[0:32, 0:32], True, True, [dtR.b, CS.b])
            tt(dt_T.t[0:L, :], p.t[0:L, 0:32], dtb_bc.t[0:L, l, :], ALU.add, [p.b, dtb_bc.b], [dt_T.b])
            act(dt_T.t[0:L, :], dt_T.t[0:L, :], AF.Exp, [dt_T.b], [dt_T.b])
            act(dt_T.t[0:L, :], dt_T.t[0:L, :], AF.Ln, [dt_T.b], [dt_T.b], bias=1.0)
            tt(a_T.t[0:L, :], dt_T.t[0:L, :], A_bc.t[0:L, l, :], ALU.mult, [dt_T.b, A_bc.b], [a_T.b])
            p = pss()
            mm(p, p.t[0:L, 0:32], U_f[0:L, 0:L], a_T.t[0:L, :], True, True, [CS.b, a_T.b])
            act(cs_T.t[0:L, :], p.t[0:L, 0:32], AF.Copy, [p.b], [cs_T.b])
            act(expcs_T.t[0:L, :], p.t[0:L, 0:32], AF.Exp, [p.b], [expcs_T.b])
            ts(ncs_T.t[0:L, :], p.t[0:L, 0:32], -1.0, None, ALU.mult, None, [p.b], [ncs_T.b])
            p2 = pss()
            mm(p2, p2.t[:, 0:32], SELL[L][0:L, :], cs_T.t[0:L, :], True, True, [CS.b, cs_T.b])
            act(declast.t[:, :], p2.t[:, 0:32], AF.Exp, [p2.b], [declast.b])
            tt(wend.t[0:L, :], p2.t[0:L, 0:32], cs_T.t[0:L, :], ALU.subtract, [p2.b, cs_T.b], [wend.b])
            act(wend.t[0:L, :], wend.t[0:L, :], AF.Exp, [wend.b], [wend.b])
            tt(wend.t[0:L, :], wend.t[0:L, :], dt_T.t[0:L, :], ALU.mult, [wend.b, dt_T.b], [wend.b])
            p3 = psb()
            for g in range(4):
                mm(p3, p3.t[0:L, g * 128:(g + 1) * 128], BT.t[:, g, cs], I_b.t[:, :], True, True, [BT.b, I_b.b])
            cp(B_T.t[0:L, :], p3.t[0:L, 0:512], [p3.b], [B_T.b])
            yield
            for g in range(4):
                hs = slice(g * 8, (g + 1) * 8)
                act(HTb.t[:], s_.HT.t[:, hs, :], AF.Copy, [s_.HT.b], [HTb.b])
                px = psb()
                for j in range(4):
                    mm(px, px.t[0:L, j * 128:(j + 1) * 128], xsT_.t[:, g * 4 + j, cs], I_f, True, True, [xsT_.b, CS.b])
                act(xs_T.t[0:L, :], px.t[0:L, 0:512], AF.Copy, [px.b], [xs_T.b])
                xv = xs_T.t[0:L, :].rearrange("p (h d) -> p h d", d=64)
                tt(xdt.t[0:L, :].rearrange("p (h d) -> p h d", d=64), xv,
                   dt_T.t[0:L, hs].unsqueeze(2).to_broadcast([L, 8, 64]), ALU.mult, [xs_T.b, dt_T.b], [xdt.b])
                tt(wx.t[0:L, :].rearrange("p (h d) -> p h d", d=64), xv,
                   wend.t[0:L, hs].unsqueeze(2).to_broadcast([L, 8, 64]), ALU.mult, [xs_T.b, wend.b], [wx.b])
                pz = psb()
                for j in range(4):
                    mm(pz, pz.t[0:L, j * 128:(j + 1) * 128], zsS.t[:, g * 4 + j, cs], I_f, True, True, [zsS.b, CS.b])
                act(zs_T.t[0:L, :], pz.t[0:L, 0:512], AF.Copy, [pz.b], [zs_T.b])
                pc = pss()
                mm(pc, pc.t[0:L, 0:L], BT.t[:, g, cs], CTT.t[:, g, cs], True, True, [BT.b, CTT.b])
                act(CBs.t[0:L, 0:L], pc.t[0:L, 0:L], AF.Copy, [pc.b], [CBs.b])
                pi = psb()
                mm(pi, pi.t[0:L, 0:512], CTT.t[:, g, cs], HTb.t[:].rearrange("p h d -> p (h d)"), True, True,
                   [CTT.b, HTb.b])
                py = psb()
                yield
                tt(CBs.t[0:L, 0:L], CBs.t[0:L, 0:L], U_f[0:L, 0:L], ALU.mult, [CBs.b, CS.b], [CBs.b])
                for hf in range(2):
                    h0 = g * 8 + hf * 4
                    n4 = 4 * L
                    a3 = A1h.t[0:L, 0:n4].rearrange("p (j t) -> p j t", t=L)
                    tt(a3, U_f[0:L, 0:L].unsqueeze(1).to_broadcast([L, 4, L]),
                       a_T.t[0:L, h0:h0 + 4].unsqueeze(2).to_broadcast([L, 4, L]), ALU.mult, [CS.b, a_T.b], [A1h.b])
                    pd = pss()
                    mm(pd, pd.t[0:L, 0:n4], ONES[0:L, 0:L], A1h.t[0:L, 0:n4], True, True, [CS.b, A1h.b])
                    for j4 in range(4):
                        ts(decTh.t[0:L, j4 * L:(j4 + 1) * L], pd.t[0:L, j4 * L:(j4 + 1) * L],
                           ncs_T.t[0:L, h0 + j4:h0 + j4 + 1], 0.0, ALU.add, ALU.min, [pd.b, ncs_T.b], [decTh.b])
                    act(decTh.t[0:L, 0:n4], decTh.t[0:L, 0:n4], AF.Exp, [decTh.b], [decTh.b])
                    tt(MTh.t[0:L, 0:n4].rearrange("p (j t) -> p j t", t=L),
                       decTh.t[0:L, 0:n4].rearrange("p (j t) -> p j t", t=L),
                       CBs.t[0:L, 0:L].unsqueeze(1).to_broadcast([L, 4, L]), ALU.mult, [decTh.b, CBs.b], [MTh.b])
                    for j4 in range(4):
                        hh_ = hf * 4 + j4
                        mm(py, py.t[0:L, hh_ * 64:(hh_ + 1) * 64], MTh.t[0:L, j4 * L:(j4 + 1) * L],
                           xdt.t[0:L, hh_ * 64:(hh_ + 1) * 64], True, True, [MTh.b, xdt.b])
                    yield
                y3 = yt.t[0:L, :].rearrange("p (h d) -> p h d", d=64)
                tt(y3, pi.t[0:L, 0:512].rearrange("p (h d) -> p h d", d=64),
                   expcs_T.t[0:L, hs].unsqueeze(2).to_broadcast([L, 8, 64]), ALU.mult, [pi.b, expcs_T.b], [yt.b])
                tt(yt.t[0:L, :], yt.t[0:L, :], py.t[0:L, 0:512], ALU.add, [yt.b, py.b], [yt.b])
                tt(yt2.t[0:L, :].rearrange("p (h d) -> p h d", d=64), xv,
                   D_bc.t[0:L, l, hs].unsqueeze(2).to_broadcast([L, 8, 64]), ALU.mult, [xs_T.b, D_bc.b], [yt2.b])
                tt(yt.t[0:L, :], yt.t[0:L, :], yt2.t[0:L, :], ALU.add, [yt.b, yt2.b], [yt.b])
                tt(yt.t[0:L, :], yt.t[0:L, :], zs_T.t[0:L, :], ALU.mult, [yt.b, zs_T.b], [yt.b])
                act(yt2.t[0:L, 0:512], yt.t[0:L, :], AF.Square, [yt.b], [yt2.b, ssq3.b], accum_out=ssq3.t[0:L, 1:2])
                act(ssq3.t[0:L, 1:2], ssq3.t[0:L, 1:2], AF.Ln, [ssq3.b], [ssq3.b], scale=1.0 / 512, bias=EPS)
                act(ssq3.t[0:L, 1:2], ssq3.t[0:L, 1:2], AF.Exp, [ssq3.b], [ssq3.b], scale=-0.5)
                ts(yo.t[0:L, :], yt.t[0:L, :], ssq3.t[0:L, 1:2], None, ALU.mult, None, [yt.b, ssq3.b], [yo.b])
                yield
                for j in range(4):
                    p9 = pss()
                    mm(p9, p9.t[:, 0:L], yo.t[0:L, j * 128:(j + 1) * 128], I_b.t[0:L, 0:L], True, True, [yo.b, I_b.b])
                    ts(mixT.t[:, 8 + g * 4 + j, cs], p9.t[:, 0:L], sng.t[:, l, g * 4 + j:g * 4 + j + 1], None, ALU.mult, None,
                       [p9.b, sng.b], [mixT.b])
                pst = psb()
                mm(pst, pst.t[:, 0:512], B_T.t[0:L, g * 128:(g + 1) * 128], wx.t[0:L, :], True, True, [B_T.b, wx.b])
                tt(s_.HT.t[:, hs, :], s_.HT.t[:, hs, :], declast.t[:, hs].unsqueeze(2).to_broadcast([128, 8, 64]), ALU.mult,
                   [s_.HT.b, declast.b], [s_.HT.b])
                tt(s_.HT.t[:, hs, :], s_.HT.t[:, hs, :], pst.t[:, 0:512].rearrange("p (h d) -> p h d", d=64), ALU.add,
                   [s_.HT.b, pst.b], [s_.HT.b])

        def cmlp_chunk(l, c0, L, cv_dst=None):
            cs = slice(c0, c0 + L)
            for hf in range(2):
                p = psb()
                for i4 in range(4):
                    i = hf * 4 + i4
                    mm(p, p.t[0:L, i4 * 128:(i4 + 1) * 128], vT.t[:, i, cs], I_f, True, True, [vT.b, CS.b])
                for g2 in range(2):
                    g = hf * 2 + g2
                    act(junk2.t[0:L, 0:256], p.t[0:L, g2 * 256:(g2 + 1) * 256], AF.Square, [p.b], [junk2.b, ssq2.b],
                        accum_out=ssq2.t[0:L, g:g + 1])
                act(ssq2.t[0:L, hf * 2:hf * 2 + 2], ssq2.t[0:L, hf * 2:hf * 2 + 2], AF.Ln, [ssq2.b], [ssq2.b],
                    scale=1.0 / 256, bias=EPS)
                act(ssq2.t[0:L, hf * 2:hf * 2 + 2], ssq2.t[0:L, hf * 2:hf * 2 + 2], AF.Exp, [ssq2.b], [ssq2.b], scale=-0.5)
                tt(vn.t[0:L, hf * 512:(hf + 1) * 512].rearrange("p (g d) -> p g d", d=256),
                   p.t[0:L, 0:512].rearrange("p (g d) -> p g d", d=256),
                   ssq2.t[0:L, hf * 2:hf * 2 + 2].unsqueeze(2).to_broadcast([L, 2, 256]), ALU.mult, [p.b, ssq2.b], [vn.b])
                yield
            tt(vn.t[0:L, :], vn.t[0:L, :], cvg_bc.t[0:L, :], ALU.mult, [vn.b, cvg_bc.b], [vn.b])
            if cv_dst is not None:
                dma("sp", cv_dst, vn.t[0:L, :], vn, R=[vn.b])
            act(vnb.t[0:L, :], vn.t[0:L, :], AF.Copy, [vn.b], [vnb.b])
            yield
            for g in range(4):
                yield
                for dt in range(2):
                    p = pss()
                    mm(p, p.t[:, 0:L], vnb.t[0:L, g * 256 + dt * 128:g * 256 + (dt + 1) * 128], WmT.t[0:L, l, g, 0:L],
                       True, True, [vnb.b, WmT.b])
                    tt(tmpc.t[:, 0:L], p.t[:, 0:L], cb_bc.t[:, l, g, 0:L], ALU.add, [p.b, cb_bc.b], [tmpc.b])
                    tt(tmpc.t[:, 0:L], tmpc.t[:, 0:L], uT.t[:, g * 2 + dt, cs], ALU.mult, [tmpc.b, uT.b], [tmpc.b])
                    tt(mixT.t[:, 24 + g * 2 + dt, cs], tmpc.t[:, 0:L], zcS.t[:, g * 2 + dt, cs], ALU.mult,
                       [tmpc.b, zcS.b], [mixT.b])

        def run_il(chains):
            recs = []
            for (pname, mk) in chains:
                c.rec = []
                cpool["p"] = CH_POOLS[pname]
                r_ = mk()
                if r_ is not None:
                    for _ in r_:
                        pass
                recs.append(c.rec)
                c.rec = None
                cpool["p"] = None
            pos = [0] * len(recs)
            total = sum(len(r) for r in recs)
            for _ in range(total):
                best, bf = None, 2.0
                for i, r in enumerate(recs):
                    if pos[i] < len(r):
                        fr = pos[i] / len(r)
                        if fr < bf:
                            best, bf = i, fr
                kind, e, fn, R, W, inc_ = recs[best][pos[best]]
                pos[best] += 1
                if kind == "op":
                    c.op(e, fn, reads=R, writes=W, inc=inc_)
                else:
                    out, in_, sbuf_, kw = fn
                    c.dma(e, out, in_, sbuf_, reads=R, writes=W, **kw)

        def mlstm_seq(l, chunks, L, s_):
            for c0 in chunks:
                yield from mlstm_chunk(l, c0, L, s_)

        def out_proj(l, NT):
            for dg in range(NDT // 2):
                ps = [pss(), pss()]
                for kh in range(2):
                    slot, wv = load_w(w_out[l][kh * 2048:(kh + 1) * 2048, dg * 256:(dg + 1) * 256], 16, 256,
                                      wscr_out[l, dg * 2 + kh])
                    for j in range(2):
                        p = ps[j]
                        for e_ in range(16):
                            mm(p, p.t[:, 0:NT], wv[:, e_, j * 128:(j + 1) * 128], mixT.t[:, kh * 16 + e_, 0:NT],
                               kh == 0 and e_ == 0, kh == 1 and e_ == 15, [slot.b, mixT.b])
                for j in range(2):
                    dtile = dg * 2 + j
                    tt(xres.t[:, dtile, 0:NT], xres.t[:, dtile, 0:NT], ps[j].t[:, 0:NT], ALU.add, [xres.b, ps[j].b], [xres.b])

        def load_qkvo(l):
            dma("sp", cvg_bc.t[:], c_v_norm_g[l].partition_broadcast(128), cvg_bc, W=[cvg_bc.b])
            dma("pool", wqk.t[:], m_w_qk[l].rearrange("h (dt p) e -> p (h dt) e", p=128), wqk, W=[wqk.b])
            dma("pool", wvo.t[:], m_w_vo[l].rearrange("h (dt p) e -> p (h dt) e", p=128), wvo, W=[wvo.b])

        def norm_to_hT(l, NT):
            rmsnorm_stats(NT, xres)
            for i in range(NDT):
                stt(hT.t[:, i, 0:NT], xres.t[:, i, 0:NT], gcol.t[:, l, i:i + 1], rstd_bc.t[:, 0:NT], ALU.mult, ALU.mult,
                    [xres.b, gcol.b, rstd_bc.b], [hT.b])

        def final_out(NT, dst):
            rmsnorm_stats(NT, xres)
            dv = dst.rearrange("(t p) c -> p t c", p=128)
            for hf, buf in enumerate((uT, vT)):
                for i in range(8):
                    k = hf * 8 + i
                    stt(buf.t[:, i, 0:NT], xres.t[:, k, 0:NT], gcol.t[:, 2, k:k + 1], rstd_bc.t[:, 0:NT], ALU.mult, ALU.mult,
                        [xres.b, gcol.b, rstd_bc.b], [buf.b])
                dma("sp", dv[:, hf * 8:(hf + 1) * 8, :], buf.t[:, :, 0:NT], buf, R=[buf.b])

        def load_state(s_, srcC, srcH, j, stg):
            dma("sp", stg.t[:].rearrange("p (a k) -> p a k", k=256), srcC.rearrange("h (vt p) k -> p (h vt) k", p=128),
                stg, W=[stg.b])
            for h in range(4):
                p = psb()
                for kt in range(2):
                    for vt in range(2):
                        mm(p, p.t[:, kt * 256 + vt * 128:kt * 256 + (vt + 1) * 128],
                           stg.t[:, (h * 2 + vt) * 256 + kt * 128:(h * 2 + vt) * 256 + (kt + 1) * 128], I_f, True, True,
                           [stg.b, CS.b])
                cp(s_.CT.t[:, h, :, 0:256], p.t[:, 0:512].rearrange("p (a v) -> p a v", v=256), [p.b], [s_.CT.b])
            cp(s_.CT.t[:, :, :, 256], nT_all.t[:, :, j * 4:(j + 1) * 4].rearrange("p kt h -> p h kt"), [nT_all.b], [s_.CT.b])
            cp(s_.m.t[:, 0:1], m_all.t[:, j:j + 1], [m_all.b], [s_.m.b])
            dma("sp", stg.t[:].rearrange("p (a n) -> p a n", n=128), srcH.rearrange("(pr hh) p n -> (hh p) pr n", hh=2),
                stg, W=[stg.b])
            for q4 in range(4):
                p = psb()
                for j2 in range(4):
                    pr = q4 * 4 + j2
                    mm(p, p.t[:, j2 * 128:(j2 + 1) * 128], stg.t[:, pr * 128:(pr + 1) * 128], I_f, True, True, [stg.b, CS.b])
                cp(s_.HT.t[:, q4 * 8:(q4 + 1) * 8, :], p.t[:, 0:512].rearrange("p (h d) -> p h d", d=64), [p.b], [s_.HT.b])

        def store_state(s_, dstC, dstn, dstm, dstH, stg):
            for h in range(4):
                p = psb()
                for vt in range(2):
                    for kt in range(2):
                        mm(p, p.t[:, vt * 256 + kt * 128:vt * 256 + (kt + 1) * 128],
                           s_.CT.t[:, h, kt, vt * 128:(vt + 1) * 128], I_f, True, True, [s_.CT.b, CS.b])
                cp(stg.t[:, h * 512:(h + 1) * 512], p.t[:, 0:512], [p.b], [stg.b])
            dma("sp", dstC.rearrange("h (vt p) k -> p (h vt) k", p=128), stg.t[:].rearrange("p (a k) -> p a k", k=256), stg,
                R=[stg.b])
            p = pss()
            for kt in range(2):
                mm(p, p.t[0:4, kt * 128:(kt + 1) * 128], s_.CT.t[:, :, kt, 256], I_f, True, True, [s_.CT.b, CS.b])
            cp(nrow.t[0:4, :], p.t[0:4, 0:256], [p.b], [nrow.b])
            dma("sp", dstn, nrow.t[0:4, :], nrow, R=[nrow.b])
            dma("sp", dstm.rearrange("(h o) -> h o", o=1), s_.m.t[:, 0:1], s_.m, R=[s_.m.b])
            for q4 in range(4):
                p = psb()
                for j2 in range(4):
                    pr = q4 * 4 + j2
                    mm(p, p.t[:, j2 * 128:(j2 + 1) * 128], s_.HT.t[:, pr * 2:pr * 2 + 2, :].rearrange("p h d -> p (h d)"), I_f,
                       True, True, [s_.HT.b, CS.b])
                cp(stg.t[:, q4 * 512:(q4 + 1) * 512], p.t[:, 0:512], [p.b], [stg.b])
            dma("sp", dstH.rearrange("(pr hh) p n -> (hh p) pr n", hh=2), stg.t[:].rearrange("p (a n) -> p a n", n=128), stg,
                R=[stg.b])

        def ck(name):
            if stage is not None and name == stage:
                raise _Stop()

        def body():
          ck("const")
          if do_sample:
              NT = NSEQ * TS
              dma("sp", xres.t[:, :, 0:NT], xsT.rearrange("(t p) c -> p t c", p=128), xres, W=[xres.b])
              for l in range(2):
                  load_qkvo(l)
                  dma("sp", nrow.t[0:64, :], sn[l], nrow, W=[nrow.b])
                  p = pss()
                  for kt in range(2):
                      mm(p, p.t[:, kt * 64:(kt + 1) * 64], nrow.t[0:64, kt * 128:(kt + 1) * 128], I_f[0:64, 0:64], True, True,
                         [nrow.b, CS.b])
                  cp(nT_all.t[:, :, :], p.t[:, 0:128].rearrange("p (kt a) -> p kt a", a=64), [p.b], [nT_all.b])
                  dma("sp", m_all.t[:, :], smT[l], m_all, W=[m_all.b])
                  for (src, buf, ntile) in ((smc, xmin, 8), (ssc, sbcin, 24)):
                      for ch in range(ntile // 8):
                          dma("sp", rowstg.t[0:48, :], src[l][:, ch * 1024:(ch + 1) * 1024], rowstg, W=[rowstg.b])
                          for i8 in range(8):
                              i = ch * 8 + i8
                              p = pss()
                              mm(p, p.t[:, 0:48], rowstg.t[0:48, i8 * 128:(i8 + 1) * 128], I_f[0:48, 0:48], True, True,
                                 [rowstg.b, CS.b])
                              cp(conv_views(buf, i, NSEQ, TS)[:, :, 0:3], p.t[:, 0:48].rearrange("p (j r) -> p j r", r=3),
                                 [p.b], [buf.b])
                  ck("hist")
                  norm_to_hT(l, NT)
                  ck("norm")
                  in_proj(l, NT, NSEQ, TS, ("xm", "zm"))
                  conv(l, NT, NSEQ, TS, ("xm",))
                  in_proj(l, NT, NSEQ, TS, ("zs", "xbc"))
                  conv(l, NT, NSEQ, TS, ("xbc",))
                  in_proj(l, NT, NSEQ, TS, ("dt", "u", "v", "zc"))
                  ck("conv")
                  s_ = S[0]
                  for j in range(NSEQ):
                      load_state(s_, sC[l, j], sssm[l, j], j, stgS)
                      ck("load")
                      run_il([("mlstm", lambda: mlstm_seq(l, [j * TS], TS, s_)),
                              ("ssd", lambda: ssd_chunk(l, j * TS, TS, s_)),
                              ("cmlp", lambda: cmlp_chunk(l, j * TS, TS, cv_dst=ocv[l, j]))])
                      ck("cmlp")
                      store_state(s_, oC[l, j], on[l, j], om[l, j], ossm[l, j], stgO)
                      ck("store")
                  conv_state_out(l, NT, "s")
                  out_proj(l, NT)
                  ck("outproj")
              final_out(NT, ysT)
              ck("sfinal")
              c.barrier()
              phase["p"] = "prompt"

          for l in range(2):
              c.op("dve", lambda e, l=l: e.memset(S[l].CT.t[:], 0.0), writes=[S[l].CT.b])
              c.op("dve", lambda e, l=l: e.memset(S[l].HT.t[:], 0.0), writes=[S[l].HT.b])
              c.op("dve", lambda e, l=l: e.memset(S[l].m.t[:], 0.0), writes=[S[l].m.b])
              c.op("dve", lambda e, l=l: e.memset(mh[l].t[:], 0.0), writes=[mh[l].b])
              c.op("dve", lambda e, l=l: e.memset(sh[l].t[:], 0.0), writes=[sh[l].b])
          NT = 128
          for b in range(NBLK):
              dma("sp", xres.t[:, :, 0:NT], xT.rearrange("(t p) c -> p t c", p=128)[:, :, b * 128:(b + 1) * 128], xres,
                  W=[xres.b])
              for l in range(2):
                  load_qkvo(l)
                  norm_to_hT(l, NT)
                  cp(xmin.t[:, :, 0:3], mh[l].t[:, :, :], [mh[l].b], [xmin.b])
                  cp(sbcin.t[:, :, 0:3], sh[l].t[:, :, :], [sh[l].b], [sbcin.b])
                  ck("p_norm")
                  in_proj(l, NT, 1, 128, ("xm", "zm"))
                  conv(l, NT, 1, 128, ("xm",))
                  in_proj(l, NT, 1, 128, ("zs", "xbc"))
                  conv(l, NT, 1, 128, ("xbc",))
                  in_proj(l, NT, 1, 128, ("dt", "u", "v", "zc"))
                  ck("p_conv")
                  cp(mh[l].t[:, :, :], xmin.t[:, :, 128:131], [xmin.b], [mh[l].b])
                  cp(sh[l].t[:, :, :], sbcin.t[:, :, 128:131], [sbcin.b], [sh[l].b])
                  if b == NBLK - 1:
                      conv_state_out(l, NT, "p")
                  run_il([("mlstm", lambda: mlstm_seq(l, [0, 64], 64, S[l])),
                          ("ssd", lambda: ssd_chunk(l, 0, 128, S[l])),
                          ("cmlp", lambda: cmlp_chunk(l, 0, 128))])
                  ck("p_cmlp")
                  out_proj(l, NT)
                  ck("p_outproj")
              final_out(NT, yT[:, b * 128:(b + 1) * 128])
          for l in range(2):
              store_state(S[l], pC[l], pn[l], pm[l], pssm[l], stgP)

        try:
            body()
        except _Stop:
            pass
        c.barrier(final=True)
        c.emit()
    return nc, c.ninst


def make_consts():
    cst = np.zeros((128, 13, 128), np.float32)
    s = np.arange(128)[:, None]
    t = np.arange(128)[None, :]
    cst[:, 0, :] = np.eye(128, dtype=np.float32)
    cst[:, 1, :] = (s <= t)
    cst[:, 2, :] = np.where(s > t, 30000.0, 0.0)
    cst[:, 3, :] = np.where(s > t, -30000.0, 0.0)
    cst[:, 4, :] = 1.0
    cst[127, 5, :] = 1.0
    cst[63, 6, :] = 1.0
    cst[3, 7, :] = 1.0
    for h in range(4):
        cst[h, 9 + h, :] = 1.0
    return cst


_NC_CACHE = {}


def kernel(x_prompt, x_sample, state_mlstm_C, state_mlstm_n, state_mlstm_m, state_mlstm_conv,
           state_ssm, state_ssm_conv, norm_g, w_in, m_conv_w, m_conv_b, m_w_qk, m_w_vo, m_b_o,
           m_w_gate, m_b_gate, m_norm_g, s_conv_w, s_conv_b, s_dt_bias, s_A_log, s_D, s_norm_g,
           c_v_norm_g, c_w_s, c_b_s, w_out, final_norm_g, _ncores=8):
    f = lambda a: np.ascontiguousarray(np.asarray(a, dtype=np.float32))
    x_prompt = f(x_prompt)
    x_sample = f(x_sample)
    BATCH, SEQ, _ = x_prompt.shape
    DEC = x_sample.shape[0]
    NBLK = SEQ // 128
    ncores = _ncores
    assert DEC == NSEQ * ncores
    if NBLK not in _NC_CACHE:
        _NC_CACHE[NBLK] = build(NBLK)[0]
    nc = _NC_CACHE[NBLK]
    col = lambda a, nt: np.ascontiguousarray(np.asarray(a, np.float32).reshape(nt, 128).T)
    ng, fg = np.asarray(norm_g, np.float32), np.asarray(final_norm_g, np.float32)
    gcol_h = np.stack([col(ng[0], 16), col(ng[1], 16), col(fg, 16)], axis=1)
    mcw, mcb = np.asarray(m_conv_w, np.float32), np.asarray(m_conv_b, np.float32)
    scw, scb = np.asarray(s_conv_w, np.float32), np.asarray(s_conv_b, np.float32)
    cwm_h = np.stack([np.stack([col(mcw[l, k], 8) for k in range(4)] + [col(mcb[l], 8)], axis=-1) for l in range(2)], axis=1)
    cws_h = np.stack([np.stack([col(scw[l, k], 24) for k in range(4)] + [col(scb[l], 24)], axis=-1) for l in range(2)], axis=1)
    mng_h = np.stack([col(np.asarray(m_norm_g)[l], 8) for l in range(2)], axis=1)
    sng_h = np.stack([col(np.asarray(s_norm_g)[l], 16) for l in range(2)], axis=1)
    mbo_h = np.stack([col(np.asarray(m_b_o)[l], 8) for l in range(2)], axis=1)
    shared = {
        "gcol_h": f(gcol_h), "cwm_h": f(cwm_h), "cws_h": f(cws_h), "mng_h": f(mng_h), "sng_h": f(sng_h), "mbo_h": f(mbo_h),
        "w_in": f(w_in), "m_w_qk": f(m_w_qk), "m_w_vo": f(m_w_vo), "m_w_gate": f(m_w_gate),
        "m_b_gate": f(m_b_gate), "s_dt_bias": f(s_dt_bias), "s_A_log": f(s_A_log), "s_D": f(s_D),
        "c_v_norm_g": f(c_v_norm_g), "c_w_sT": f(np.transpose(np.asarray(c_w_s), (0, 1, 3, 2))),
        "c_b_s": f(c_b_s), "w_out": f(w_out), "cst": make_consts(),
    }
    sC_, sn_, sm_, smc_, sss_, ssc_ = (f(state_mlstm_C), f(state_mlstm_n), f(state_mlstm_m), f(state_mlstm_conv),
                                       f(state_ssm), f(state_ssm_conv))
    in_maps = []
    for cid in range(ncores):
        sl = slice(cid * NSEQ, (cid + 1) * NSEQ)
        m = dict(shared)
        m["xT"] = f(x_prompt[cid % BATCH].T)
        m["xsT"] = f(x_sample[sl].reshape(NSEQ * TS, D).T)
        m["sC"] = f(sC_[:, sl])
        m["sn"] = f(sn_[:, sl].reshape(2, NSEQ * 4, 256))
        m["smT"] = f(np.transpose(sm_[:, sl], (0, 2, 1)))
        m["smc"] = f(smc_[:, sl].reshape(2, NSEQ * 3, 1024))
        m["sssm"] = f(sss_[:, sl])
        m["ssc"] = f(ssc_[:, sl].reshape(2, NSEQ * 3, 3072))
        in_maps.append(m)
    res = run_bass_kernel_spmd(nc, in_maps, core_ids=list(range(ncores)))
    R = res.results
    nb = min(BATCH, ncores)
    y_prompt = np.stack([R[i]["yT"].T for i in range(nb)])
    y_sample = np.concatenate([R[i]["ysT"].T.reshape(NSEQ, TS, D) for i in range(ncores)])
    pst = lambda k: np.stack([R[i][k] for i in range(nb)], axis=1)
    sst = lambda k: np.concatenate([R[i][k] for i in range(ncores)], axis=1)
    outs = (y_prompt, y_sample, pst("pC"), pst("pn"), pst("pm"), pst("pmc"), pst("pssm"), pst("psc"),
            sst("oC"), sst("on"), sst("om"), sst("omc"), sst("ossm"), sst("osc"), sst("ocv"))
    return tuple(np.ascontiguousarray(o, dtype=np.float32) for o in outs)
```
